# Optimizing a Trainium2 kernel written in Bass

```python
import jax, jax.numpy as jnp
from jax import lax
import numpy as np

D_MODEL = 1024
BATCH = 16
SEQ = 2048
DEPTH = 1
DEC_BATCH = 16
DEC_SEQ = 64
PAST_LEN = 4096

CHUNK = 64
HEAD_DIM = 64
SB_HEADS = 8
FOX_HEADS = 8
SB_WIDTH = SB_HEADS * HEAD_DIM
FOX_WIDTH = FOX_HEADS * HEAD_DIM
MIX_WIDTH = SB_WIDTH + FOX_WIDTH
IN_COLS = 3 * SB_WIDTH + 3 * FOX_WIDTH + FOX_HEADS
QBLOCK = 128
N_MEM = 256
MEM_HEADS = 4
MEM_HEAD_DIM = D_MODEL // MEM_HEADS
PEER_HEADS = 8
PEER_NKEYS = 128
PEER_EXPERTS = PEER_NKEYS * PEER_NKEYS
PEER_TOPK = 16
PEER_QDIM = 256
PEER_HALF = PEER_QDIM // 2
TOK_BLOCK = 128
DN_ALPHA = (2.0 * DEPTH) ** 0.25
DN_BETA = (8.0 * DEPTH) ** -0.25
LN_EPS = 1e-5
GN_EPS = 1e-6

kernel_name = 'stickbreak_fox_peer_stream_encoder_step'

F32 = jnp.float32


def _layer_norm(x, g, b):
    xf = x.astype(F32)
    mu = jnp.mean(xf, axis=-1, keepdims=True)
    var = jnp.mean(jnp.square(xf - mu), axis=-1, keepdims=True)
    return ((xf - mu) * lax.rsqrt(var + LN_EPS) * g.astype(F32) + b.astype(F32)).astype(x.dtype)


def _group_rms(o, g):
    return o * lax.rsqrt(jnp.mean(jnp.square(o), axis=-1, keepdims=True) + GN_EPS) * g.astype(F32)


def _query_blocks(q, q_pos):
    B, Tq, H, dh = q.shape
    qb = min(QBLOCK, Tq)
    nb = Tq // qb
    qs = q.astype(F32).reshape(B, nb, qb, H, dh).transpose(1, 0, 2, 3, 4)
    return qs, q_pos.reshape(nb, qb), nb, qb


def _stick_breaking_attn(q, k, v, q_pos, k_pos):
    B, Tq, H, dh = q.shape
    qs, ps, nb, qb = _query_blocks(q, q_pos)
    kf = k.astype(F32)
    vf = v.astype(F32)
    scale = HEAD_DIM ** -0.5

    def one(args):
        qblk, pblk = args
        z = jnp.einsum('bqhd,bkhd->bhqk', qblk, kf) * scale
        mask = (k_pos[None, :] < pblk[:, None])[None, None]
        log_keep = jnp.where(mask, -jax.nn.softplus(z), 0.0)
        later = lax.cumsum(log_keep, axis=3, reverse=True) - log_keep
        w = jnp.where(mask, jnp.exp(jax.nn.log_sigmoid(z) + later), 0.0)
        return jnp.einsum('bhqk,bkhd->bqhd', w, vf)

    o = lax.map(one, (qs, ps))
    return o.transpose(1, 0, 2, 3, 4).reshape(B, Tq, H * dh)


def _forgetting_attn(q, k, v, cq, ck, q_pos, k_pos):
    B, Tq, H, dh = q.shape
    qs, ps, nb, qb = _query_blocks(q, q_pos)
    cqs = cq.reshape(B, nb, qb, H).transpose(1, 0, 3, 2)
    ckT = ck.transpose(0, 2, 1)
    kf = k.astype(F32)
    vf = v.astype(F32)
    scale = HEAD_DIM ** -0.5

    def one(args):
        qblk, pblk, cblk = args
        s = jnp.einsum('bqhd,bkhd->bhqk', qblk, kf) * scale + cblk[..., None] - ckT[:, :, None, :]
        mask = (k_pos[None, :] <= pblk[:, None])[None, None]
        p = jax.nn.softmax(jnp.where(mask, s, -jnp.inf), axis=-1)
        return jnp.einsum('bhqk,bkhd->bqhd', p, vf)

    o = lax.map(one, (qs, ps, cqs))
    return o.transpose(1, 0, 2, 3, 4).reshape(B, Tq, H * dh)


def _mixer(x, sb_k_past, sb_v_past, fx_k_past, fx_v_past, fx_logf_past, w_in, b_f, w_gn, w_out):
    B, T, _ = x.shape
    P = sb_k_past.shape[1]
    proj = x @ w_in
    cuts = [SB_WIDTH, 2 * SB_WIDTH, 3 * SB_WIDTH, 3 * SB_WIDTH + FOX_WIDTH,
            3 * SB_WIDTH + 2 * FOX_WIDTH, 3 * SB_WIDTH + 3 * FOX_WIDTH]
    q_sb, k_sb, v_sb, q_fx, k_fx, v_fx, f_logit = jnp.split(proj, cuts, axis=-1)
    hs = lambda a, h: a.reshape(B, T, h, HEAD_DIM)
    q_sb, k_sb, v_sb = hs(q_sb, SB_HEADS), hs(k_sb, SB_HEADS), hs(v_sb, SB_HEADS)
    q_fx, k_fx, v_fx = hs(q_fx, FOX_HEADS), hs(k_fx, FOX_HEADS), hs(v_fx, FOX_HEADS)
    logf = jax.nn.log_sigmoid((f_logit + b_f).astype(F32))

    k_pos = jnp.arange(P + T, dtype=jnp.int32)
    q_pos = P + jnp.arange(T, dtype=jnp.int32)
    k_sb_all = jnp.concatenate([sb_k_past.astype(x.dtype), k_sb], axis=1)
    v_sb_all = jnp.concatenate([sb_v_past.astype(x.dtype), v_sb], axis=1)
    k_fx_all = jnp.concatenate([fx_k_past.astype(x.dtype), k_fx], axis=1)
    v_fx_all = jnp.concatenate([fx_v_past.astype(x.dtype), v_fx], axis=1)
    c_all = lax.cumsum(jnp.concatenate([fx_logf_past.astype(F32), logf], axis=1), axis=1)

    o_sb = _stick_breaking_attn(q_sb, k_sb_all, v_sb_all, q_pos, k_pos)
    o_fx = _forgetting_attn(q_fx, k_fx_all, v_fx_all, c_all[:, P:], c_all, q_pos, k_pos)
    o = jnp.concatenate([_group_rms(o_sb, w_gn[:SB_WIDTH]), _group_rms(o_fx, w_gn[SB_WIDTH:])], axis=-1)
    y = o.astype(x.dtype) @ w_out
    return y, (k_sb, v_sb, k_fx, v_fx, logf.astype(x.dtype))


def _mem_kv(mem, w_mk, w_mv):
    B, M, _ = mem.shape
    mk = (mem @ w_mk).reshape(B, M, MEM_HEADS, MEM_HEAD_DIM)
    mv = (mem @ w_mv).reshape(B, M, MEM_HEADS, MEM_HEAD_DIM)
    return mk, mv


def _mem_attn(x, mk, mv, w_mq, w_mo):
    B, T, _ = x.shape
    q = (x @ w_mq).reshape(B, T, MEM_HEADS, MEM_HEAD_DIM).astype(F32)
    s = jnp.einsum('bthd,bmhd->bhtm', q, mk.astype(F32)) * (MEM_HEAD_DIM ** -0.5)
    p = jax.nn.softmax(s, axis=-1)
    o = jnp.einsum('bhtm,bmhd->bthd', p, mv.astype(F32)).reshape(B, T, D_MODEL)
    return o.astype(x.dtype) @ w_mo


def _peer(x, w_pq, keys_a, keys_b, u, v):
    B, T, D = x.shape
    n = B * T
    xf = x.reshape(n, D)
    q = (xf @ w_pq).astype(F32).reshape(n, PEER_HEADS, 2, PEER_HALF)
    sa = jnp.einsum('nhd,hkd->nhk', q[:, :, 0], keys_a.astype(F32))
    sb = jnp.einsum('nhd,hkd->nhk', q[:, :, 1], keys_b.astype(F32))
    va, ia = lax.top_k(sa, PEER_TOPK)
    vb, ib = lax.top_k(sb, PEER_TOPK)
    cand = (va[..., :, None] + vb[..., None, :]).reshape(n, PEER_HEADS, PEER_TOPK * PEER_TOPK)
    cidx = (ia[..., :, None] * PEER_NKEYS + ib[..., None, :]).reshape(n, PEER_HEADS, PEER_TOPK * PEER_TOPK)
    top, pos = lax.top_k(cand, PEER_TOPK)
    idx = jnp.take_along_axis(cidx, pos, axis=-1)
    g = jax.nn.softmax(top, axis=-1)

    blk = min(TOK_BLOCK, n)
    pad = (-n) % blk
    nb = (n + pad) // blk
    xb = jnp.pad(xf, ((0, pad), (0, 0))).reshape(nb, blk, D)
    ibk = jnp.pad(idx, ((0, pad), (0, 0), (0, 0))).reshape(nb, blk, PEER_HEADS, PEER_TOPK)
    gbk = jnp.pad(g, ((0, pad), (0, 0), (0, 0))).reshape(nb, blk, PEER_HEADS, PEER_TOPK)

    def one(args):
        xx, ii, gg = args
        ue = jnp.take(u, ii, axis=0).astype(F32)
        h = jax.nn.gelu(jnp.einsum('nd,nhkd->nhk', xx.astype(F32), ue), approximate=False) * gg
        ve = jnp.take(v, ii, axis=0).astype(F32)
        return jnp.einsum('nhk,nhkd->nd', h, ve)

    out = lax.map(one, (xb, ibk, gbk)).reshape(nb * blk, D)[:n]
    return out.reshape(B, T, D).astype(x.dtype)


def _layer(x, sb_k_past, sb_v_past, fx_k_past, fx_v_past, fx_logf_past, mem_k, mem_v,
           w_in, b_f, w_gn, w_out, ln1_g, ln1_b, w_mq, w_mo, ln2_g, ln2_b,
           w_pq, keys_a, keys_b, u, v, ln3_g, ln3_b):
    mix, new_state = _mixer(x, sb_k_past, sb_v_past, fx_k_past, fx_v_past, fx_logf_past, w_in, b_f, w_gn, w_out)
    x = _layer_norm(DN_ALPHA * x + mix, ln1_g, ln1_b)
    x = _layer_norm(DN_ALPHA * x + _mem_attn(x, mem_k, mem_v, w_mq, w_mo), ln2_g, ln2_b)
    x = _layer_norm(DN_ALPHA * x + _peer(x, w_pq, keys_a, keys_b, u, v), ln3_g, ln3_b)
    return x, new_state


def setup_inputs(seed: int = 0) -> dict:
    key = jax.random.key(seed)
    ks = jax.random.split(key, 40)
    nrm = lambda k, shape, s: jax.random.normal(k, shape, F32) * s
    L = DEPTH
    kv_shape = (L, DEC_BATCH, PAST_LEN, SB_HEADS, HEAD_DIM)
    fx_shape = (L, DEC_BATCH, PAST_LEN, FOX_HEADS, HEAD_DIM)
    mem_shape = (L, DEC_BATCH, N_MEM, MEM_HEADS, MEM_HEAD_DIM)
    return {
        'x_prompt': nrm(ks[0], (BATCH, SEQ, D_MODEL), 1.0),
        'x_sample': nrm(ks[1], (DEC_BATCH, DEC_SEQ, D_MODEL), 1.0),
        'mem_prompt': nrm(ks[2], (BATCH, N_MEM, D_MODEL), 1.0),
        'cache_sb_k': nrm(ks[3], kv_shape, 1.0),
        'cache_sb_v': nrm(ks[4], kv_shape, 1.0),
        'cache_fox_k': nrm(ks[5], fx_shape, 1.0),
        'cache_fox_v': nrm(ks[6], fx_shape, 1.0),
        'cache_fox_logf': jax.nn.log_sigmoid(3.0 + nrm(ks[7], (L, DEC_BATCH, PAST_LEN, FOX_HEADS), 1.0)),
        'cache_mem_k': nrm(ks[8], mem_shape, 1.0),
        'cache_mem_v': nrm(ks[9], mem_shape, 1.0),
        'w_in': nrm(ks[10], (L, D_MODEL, IN_COLS), D_MODEL ** -0.5),
        'b_f': 3.0 + nrm(ks[11], (L, FOX_HEADS), 0.5),
        'w_gn': 1.0 + nrm(ks[12], (L, MIX_WIDTH), 0.02),
        'w_out': nrm(ks[13], (L, MIX_WIDTH, D_MODEL), DN_BETA * MIX_WIDTH ** -0.5),
        'ln1_g': 1.0 + nrm(ks[14], (L, D_MODEL), 0.02),
        'ln1_b': nrm(ks[15], (L, D_MODEL), 0.02),
        'w_mq': nrm(ks[16], (L, D_MODEL, D_MODEL), D_MODEL ** -0.5),
        'w_mk': nrm(ks[17], (L, D_MODEL, D_MODEL), D_MODEL ** -0.5),
        'w_mv': nrm(ks[18], (L, D_MODEL, D_MODEL), D_MODEL ** -0.5),
        'w_mo': nrm(ks[19], (L, D_MODEL, D_MODEL), DN_BETA * D_MODEL ** -0.5),
        'ln2_g': 1.0 + nrm(ks[20], (L, D_MODEL), 0.02),
        'ln2_b': nrm(ks[21], (L, D_MODEL), 0.02),
        'w_pq': nrm(ks[22], (L, D_MODEL, PEER_HEADS * PEER_QDIM), D_MODEL ** -0.5),
        'peer_keys_a': nrm(ks[23], (L, PEER_HEADS, PEER_NKEYS, PEER_HALF), PEER_HALF ** -0.5),
        'peer_keys_b': nrm(ks[24], (L, PEER_HEADS, PEER_NKEYS, PEER_HALF), PEER_HALF ** -0.5),
        'peer_u': nrm(ks[25], (L, PEER_EXPERTS, D_MODEL), D_MODEL ** -0.5),
        'peer_v': nrm(ks[26], (L, PEER_EXPERTS, D_MODEL), DN_BETA * PEER_HEADS ** -0.5),
        'ln3_g': 1.0 + nrm(ks[27], (L, D_MODEL), 0.02),
        'ln3_b': nrm(ks[28], (L, D_MODEL), 0.02),
    }


def reference(x_prompt, x_sample, mem_prompt, cache_sb_k, cache_sb_v, cache_fox_k, cache_fox_v,
              cache_fox_logf, cache_mem_k, cache_mem_v, w_in, b_f, w_gn, w_out, ln1_g, ln1_b,
              w_mq, w_mk, w_mv, w_mo, ln2_g, ln2_b, w_pq, peer_keys_a, peer_keys_b, peer_u, peer_v,
              ln3_g, ln3_b):
    hp = x_prompt
    hs = x_sample
    Bp = x_prompt.shape[0]
    p_sbk, p_sbv, p_fxk, p_fxv, p_lf, p_mk, p_mv = [], [], [], [], [], [], []
    s_sbk, s_sbv, s_fxk, s_fxv, s_lf = [], [], [], [], []
    for l in range(DEPTH):
        weights = (w_in[l], b_f[l], w_gn[l], w_out[l], ln1_g[l], ln1_b[l], w_mq[l], w_mo[l], ln2_g[l], ln2_b[l],
                   w_pq[l], peer_keys_a[l], peer_keys_b[l], peer_u[l], peer_v[l], ln3_g[l], ln3_b[l])
        mk_p, mv_p = _mem_kv(mem_prompt, w_mk[l], w_mv[l])
        e_sb = jnp.zeros((Bp, 0, SB_HEADS, HEAD_DIM), hp.dtype)
        e_fx = jnp.zeros((Bp, 0, FOX_HEADS, HEAD_DIM), hp.dtype)
        e_lf = jnp.zeros((Bp, 0, FOX_HEADS), F32)
        hp, st_p = _layer(hp, e_sb, e_sb, e_fx, e_fx, e_lf, mk_p, mv_p, *weights)
        hs, st_s = _layer(hs, cache_sb_k[l], cache_sb_v[l], cache_fox_k[l], cache_fox_v[l], cache_fox_logf[l],
                          cache_mem_k[l], cache_mem_v[l], *weights)
        p_sbk.append(st_p[0]); p_sbv.append(st_p[1]); p_fxk.append(st_p[2]); p_fxv.append(st_p[3]); p_lf.append(st_p[4])
        p_mk.append(mk_p); p_mv.append(mv_p)
        s_sbk.append(st_s[0]); s_sbv.append(st_s[1]); s_fxk.append(st_s[2]); s_fxv.append(st_s[3]); s_lf.append(st_s[4])
    y_prompt = hp
    y_sample = hs
    new_sb_k_p = jnp.stack(p_sbk)
    new_sb_v_p = jnp.stack(p_sbv)
    new_fox_k_p = jnp.stack(p_fxk)
    new_fox_v_p = jnp.stack(p_fxv)
    new_fox_logf_p = jnp.stack(p_lf)
    new_mem_k_p = jnp.stack(p_mk)
    new_mem_v_p = jnp.stack(p_mv)
    new_sb_k_s = jnp.stack(s_sbk)
    new_sb_v_s = jnp.stack(s_sbv)
    new_fox_k_s = jnp.stack(s_fxk)
    new_fox_v_s = jnp.stack(s_fxv)
    new_fox_logf_s = jnp.stack(s_lf)
    return (y_prompt, y_sample, new_sb_k_p, new_sb_v_p, new_fox_k_p, new_fox_v_p, new_fox_logf_p,
            new_mem_k_p, new_mem_v_p, new_sb_k_s, new_sb_v_s, new_fox_k_s, new_fox_v_s, new_fox_logf_s)
```

```python
import os
import numpy as np
from contextlib import ExitStack
import concourse.bass as bass
import concourse.mybir as mybir
from concourse.bass_utils import run_bass_kernel_spmd

F32 = mybir.dt.float32
BF16 = mybir.dt.bfloat16
U32 = mybir.dt.uint32
I32 = mybir.dt.int32
AF = mybir.ActivationFunctionType
ALU = mybir.AluOpType
AX = mybir.AxisListType

D = 1024
HD = 64
NMEM = 256
ALPHA = 2.0 ** 0.25
LN_EPS = 1e-5
GN_EPS = 1e-6
NEXP = 16384


class DSem:
    __slots__ = ("sem", "total")

    def __init__(self, sem):
        self.sem = sem
        self.total = 0


class Tok:
    __slots__ = ("w", "r", "dw", "dr", "own", "own_sw", "name", "excl")

    def __init__(self, name=""):
        self.w = None
        self.r = {}
        self.dw = []
        self.dr = []
        self.own = None
        self.own_sw = None
        self.name = name
        self.excl = False


class Buf:
    __slots__ = ("t", "k")

    def __init__(self, t, k):
        self.t = t
        self.k = k


class Prog:
    def __init__(self, nc):
        self.nc = nc
        self.es = ExitStack()
        self.eng = {"pe": nc.tensor, "act": nc.scalar, "dve": nc.vector,
                    "pool": nc.gpsimd, "sp": nc.sync}
        self.sem = {}
        self.cnt = {}
        self.waited = {}
        for k in self.eng:
            self.sem[k] = self.es.enter_context(nc.semaphore("c_" + k))
            self.cnt[k] = 0
            self.waited[k] = {}
        self.dsems = []
        self.ninst = 0
        self.uid = 0
        self.limit = None
        self.marks = []

    def mark(self, name):
        self.marks.append((name, self.ninst))

    def sb(self, es, name, shape, dt, dma=False):
        self.uid += 1
        t = es.enter_context(self.nc.sbuf_tensor("%s_%d" % (name, self.uid), list(shape), dt))
        return Buf(t, self.dtok(name) if dma else Tok(name))

    def ps(self, es, name, shape, dt):
        self.uid += 1
        t = es.enter_context(self.nc.psum_tensor("%s_%d" % (name, self.uid), list(shape), dt))
        b = Buf(t, Tok(name))
        b.k.excl = True
        return b

    def dtok(self, name=""):
        t = Tok(name)
        self.uid += 1
        t.own = DSem(self.es.enter_context(self.nc.semaphore("d%d" % self.uid)))
        self.dsems.append(t.own)
        return t

    def _need(self, e, key, sem, val):
        if e == "pe" and key == "pe":
            return
        w = self.waited[e]
        if w.get(key, 0) >= val:
            return
        w[key] = val
        self.eng[e].wait_ge(sem, val)

    def _deps(self, e, reads, writes):
        for t in reads:
            if t.w is not None:
                self._need(e, t.w[0], self.sem[t.w[0]], t.w[1])
            for d in t.dw:
                self._need(e, id(d), d.sem, d.total)
        for t in writes:
            if t.w is not None:
                self._need(e, t.w[0], self.sem[t.w[0]], t.w[1])
            for re_, rc in t.r.items():
                self._need(e, re_, self.sem[re_], rc)
            for d in t.dw + t.dr:
                self._need(e, id(d), d.sem, d.total)

    def op(self, e, fn, reads=(), writes=()):
        if self.limit is not None and self.ninst >= self.limit:
            return None
        ex = [t for t in reads if t.excl]
        if ex:
            reads = [t for t in reads if not t.excl]
            writes = list(writes) + [t for t in ex if t not in writes]
        self._deps(e, reads, writes)
        ins = fn()
        self.cnt[e] += 1
        c = self.cnt[e]
        ins.then_inc(self.sem[e], 1)
        for t in writes:
            t.w = (e, c)
            t.r = {}
            t.dw = []
            t.dr = []
        for t in reads:
            if t.r.get(e, 0) < c:
                t.r[e] = c
        self.ninst += 1
        return ins

    def dma(self, q, out, in_, semtok, reads=(), writes=(), **kw):
        if self.limit is not None and self.ninst >= self.limit:
            return None
        self._deps(q, reads, writes)
        ins = self.eng[q].dma_start(out=out, in_=in_, **kw)
        if q == "pool":
            if semtok.own_sw is None:
                self.uid += 1
                semtok.own_sw = DSem(self.es.enter_context(self.nc.semaphore("w%d" % self.uid)))
                self.dsems.append(semtok.own_sw)
            d = semtok.own_sw
        else:
            d = semtok.own
        ins.then_inc(d.sem, 16)
        d.total += 16
        for t in writes:
            t.w = None
            t.r = {}
            t.dw = [d]
            t.dr = []
        if not writes and semtok.dw and d not in semtok.dw:
            semtok.dw.append(d)
        for t in reads:
            if d not in t.dr:
                t.dr.append(d)
        self.ninst += 1
        return ins

    def barrier(self):
        for e in self.eng:
            for k in ("pe", "act", "dve", "pool"):
                if self.cnt[k]:
                    self._need(e, k, self.sem[k], self.cnt[k])
            for d in self.dsems:
                if d.total:
                    self._need(e, id(d), d.sem, d.total)


def build(NP, T, NS, TS, PAST, stages="UAaMB", dbg=False):
    assert T % 128 == 0 and PAST % 128 == 0 and NS * TS == 128 and TS == 64
    nc = bass.Bass("TRN2", target_bir_lowering=False)
    P = Prog(nc)
    if os.environ.get("KLIMIT"):
        P.limit = int(os.environ["KLIMIT"])
    op, dma = P.op, P.dma
    V, A, G, PE = nc.vector, nc.scalar, nc.gpsimd, nc.tensor
    KT = T // 128
    PT_ = PAST // 128
    NTOK = NP * T + NS * TS
    NTILE = NTOK // 128
    scale = HD ** -0.5

    def din(name, shape):
        return nc.dram_tensor(name, list(shape), F32, kind="ExternalInput").ap()

    def dout(name, shape):
        return nc.dram_tensor(name, list(shape), F32, kind="ExternalOutput").ap()

    def dscr(name, shape, dt):
        return nc.dram_tensor(name, list(shape), dt, kind="Internal").ap()

    x_p = din("x_p", [NP, T, D]); x_s = din("x_s", [NS * TS, D]); mem_p = din("mem_p", [NP, NMEM, D])
    c_sbk = din("c_sbk", [NS, PAST, 512]); c_sbv = din("c_sbv", [NS, PAST, 512])
    c_fxk = din("c_fxk", [NS, PAST, 512]); c_fxv = din("c_fxv", [NS, PAST, 512])
    c_lf = din("c_lf", [NS, PAST, 8]); c_mk = din("c_mk", [NS, NMEM, D]); c_mv = din("c_mv", [NS, NMEM, D])
    w_in = din("w_in", [D, 3080]); b_f = din("b_f", [1, 8]); w_gn = din("w_gn", [D, 1]); w_out = din("w_out", [D, D])
    ln1_g = din("ln1_g", [1, D]); ln1_b = din("ln1_b", [1, D]); w_mq = din("w_mq", [D, D]); w_mk = din("w_mk", [D, D])
    w_mv = din("w_mv", [D, D]); w_mo = din("w_mo", [D, D]); ln2_g = din("ln2_g", [1, D]); ln2_b = din("ln2_b", [1, D])
    w_pq = din("w_pq", [D, 2048]); keys_a = din("keys_a", [8, 128, 128]); keys_b = din("keys_b", [8, 128, 128])
    pu = din("pu", [NEXP, D]); pv = din("pv", [NEXP, D]); ln3_g = din("ln3_g", [1, D]); ln3_b = din("ln3_b", [1, D])

    y_p = dout("y_p", [NP * T, D]); y_s = dout("y_s", [NS * TS, D])
    o_sbk_p = dout("o_sbk_p", [NP * T, 512]); o_sbv_p = dout("o_sbv_p", [NP * T, 512])
    o_fxk_p = dout("o_fxk_p", [NP * T, 512]); o_fxv_p = dout("o_fxv_p", [NP * T, 512])
    o_lf_p = dout("o_lf_p", [NP * T, 8]); o_mk_p = dout("o_mk_p", [NP * NMEM, D]); o_mv_p = dout("o_mv_p", [NP * NMEM, D])
    o_sbk_s = dout("o_sbk_s", [NS * TS, 512]); o_sbv_s = dout("o_sbv_s", [NS * TS, 512])
    o_fxk_s = dout("o_fxk_s", [NS * TS, 512]); o_fxv_s = dout("o_fxv_s", [NS * TS, 512])
    o_lf_s = dout("o_lf_s", [NS * TS, 8])

    uT_s = dscr("uT_s", [128, 128, 1024], BF16)
    v_s = dscr("v_s", [NEXP, D], BF16)
    O_s = (dout if dbg else (lambda n, sh: dscr(n, sh, F32)))("O_s", [NTOK, D])
    x2_s = (dout if dbg else (lambda n, sh: dscr(n, sh, F32)))("x2_s", [NTOK, D])
    t_Os = P.dtok("O_s"); t_x2s = P.dtok("x2_s"); t_uTs = P.dtok("uT_s"); t_vs = P.dtok("v_s")
    t_out = P.dtok("outs")

    ces = P.es
    idf = P.sb(ces, "idf", [128, 128], F32); idb = P.sb(ces, "idb", [128, 128], BF16)
    m_lt_f = P.sb(ces, "m_lt_f", [128, 128], F32); m_lt_b = P.sb(ces, "m_lt_b", [128, 128], BF16)
    m_ge_f = P.sb(ces, "m_ge_f", [128, 128], F32); m_ge_b = P.sb(ces, "m_ge_b", [128, 128], BF16)
    ones_f = P.sb(ces, "ones_f", [128, 128], F32)
    iota_i = P.sb(ces, "iota_i", [128, 128], I32); iota_f = P.sb(ces, "iota_f", [128, 128], F32)
    op("pool", lambda: G.memset(idf.t[:], 0.0), writes=[idf.k])
    op("pool", lambda: G.affine_select(idf.t[:], idf.t[:], [[-1, 128]], ALU.not_equal, 1.0, base=0, channel_multiplier=1),
       reads=[idf.k], writes=[idf.k])
    op("dve", lambda: V.tensor_copy(idb.t[:], idf.t[:]), reads=[idf.k], writes=[idb.k])
    op("pool", lambda: G.memset(ones_f.t[:], 1.0), writes=[ones_f.k])
    op("pool", lambda: G.affine_select(m_lt_f.t[:], ones_f.t[:], [[-1, 128]], ALU.is_gt, 0.0, base=0, channel_multiplier=1),
       reads=[ones_f.k], writes=[m_lt_f.k])
    op("dve", lambda: V.tensor_copy(m_lt_b.t[:], m_lt_f.t[:]), reads=[m_lt_f.k], writes=[m_lt_b.k])
    op("pool", lambda: G.affine_select(m_ge_f.t[:], ones_f.t[:], [[1, 128]], ALU.is_ge, 0.0, base=0, channel_multiplier=-1),
       reads=[ones_f.k], writes=[m_ge_f.k])
    op("dve", lambda: V.tensor_copy(m_ge_b.t[:], m_ge_f.t[:]), reads=[m_ge_f.k], writes=[m_ge_b.k])
    op("pool", lambda: G.iota(iota_i.t[:], [[1, 128]], base=0, channel_multiplier=0), writes=[iota_i.k])
    op("dve", lambda: V.tensor_copy(iota_f.t[:], iota_i.t[:]), reads=[iota_i.k], writes=[iota_f.k])

    flip = [0]

    def evac(out_ap, in_ap, reads, writes):
        flip[0] ^= 1
        if flip[0]:
            op("act", lambda: A.copy(out_ap, in_ap), reads=reads, writes=writes)
        else:
            op("dve", lambda: V.tensor_copy(out_ap, in_ap), reads=reads, writes=writes)

    def load_w_bf16(dst, src, ncols, extra_reads=()):
        c0 = 0
        first = True
        while c0 < ncols:
            c1 = min(ncols, c0 + 1024)
            for k0 in range(0, 8, 2):
                dma("pool", dst.t[:, k0:k0 + 2, c0:c1], src[k0 * 128:(k0 + 2) * 128, c0:c1].rearrange("(k p) c -> p k c", p=128), dst.k,
                    writes=[dst.k] if first else [])
                first = False
            c0 = c1

    def layer_norm(es_tmp, r, g_b, b_b, outb, nrows=128):
        st = lnw["st"]; mv = lnw["mv"]; rs = lnw["rs"]
        for c in range(2):
            op("dve", lambda c=c: V.bn_stats(st.t[:, c, :], r.t[:, c * 512:(c + 1) * 512]), reads=[r.k], writes=[st.k])
        op("dve", lambda: V.bn_aggr(mv.t[:], st.t[:]), reads=[st.k], writes=[mv.k])
        op("dve", lambda: V.tensor_scalar(rs.t[:], mv.t[:, 1:2], LN_EPS, None, ALU.add), reads=[mv.k], writes=[rs.k])
        op("act", lambda: A.sqrt(rs.t[:], rs.t[:]), reads=[rs.k], writes=[rs.k])
        op("dve", lambda: V.reciprocal(rs.t[:], rs.t[:]), reads=[rs.k], writes=[rs.k])
        op("dve", lambda: V.tensor_scalar(r.t[:], r.t[:], mv.t[:, 0:1], rs.t[:], ALU.subtract, ALU.mult),
           reads=[r.k, mv.k, rs.k], writes=[r.k])
        op("dve", lambda: V.tensor_mul(r.t[:], r.t[:], g_b.t[:]), reads=[r.k, g_b.k], writes=[r.k])
        op("dve", lambda: V.tensor_add(outb.t[:], r.t[:], b_b.t[:]), reads=[r.k, b_b.k], writes=[outb.k])

    lnw = {}
    lnw["st"] = P.sb(ces, "ln_st", [128, 2, 6], F32)
    lnw["mv"] = P.sb(ces, "ln_mv", [128, 2], F32)
    lnw["rs"] = P.sb(ces, "ln_rs", [128, 1], F32)

    def u_gen(es, banks, NB):
        ust = [P.sb(es, "ust", [128, NB, D], F32, dma=True) for _ in range(2)]
        uTb = [P.sb(es, "uTb", [128, NB, 8, 128], BF16, dma=True) for _ in range(2)]
        vb = [P.sb(es, "vb", [128, NB, D], BF16, dma=True) for _ in range(2)]
        pi = 0
        ngrp = 128 // NB

        def loads(g):
            s = g % 2
            rows = slice(g * NB * 128, (g + 1) * NB * 128)
            dma("sp", ust[s].t[:], pu[rows, :].rearrange("(b e) d -> e b d", e=128), ust[s].k, writes=[ust[s].k])
            dma("pool", vb[s].t[:], pv[rows, :].rearrange("(b e) d -> e b d", e=128), vb[s].k, writes=[vb[s].k])

        loads(0)
        for g in range(ngrp):
            s = g % 2
            rows = slice(g * NB * 128, (g + 1) * NB * 128)
            if g + 1 < ngrp:
                loads(g + 1)
            yield
            for b in range(NB):
                for hf in range(2):
                    pb = banks[pi % len(banks)]; pi += 1
                    for i in range(4):
                        k = hf * 4 + i
                        op("pe", lambda pb=pb, i=i, k=k, b=b: PE.transpose(pb.t[:, i * 128:(i + 1) * 128],
                                                                          ust[s].t[:, b, k * 128:(k + 1) * 128], idf.t[:]),
                           reads=[ust[s].k, idf.k], writes=[pb.k])
                    evac(uTb[s].t[:, b, hf * 4:(hf + 1) * 4, :], pb.t[:].rearrange("p (k e) -> p k e", e=128),
                         reads=[pb.k], writes=[uTb[s].k])
                    yield
            dma("sp", v_s[rows, :].rearrange("(b e) d -> e b d", e=128), vb[s].t[:], vb[s].k, reads=[vb[s].k], writes=[t_vs])
            yield
            dma("sp", uT_s[g * NB:(g + 1) * NB].rearrange("b p f -> p b f"),
                uTb[s].t[:].rearrange("p b k e -> p b (k e)"), uTb[s].k, reads=[uTb[s].k], writes=[t_uTs])
            yield

    u_overlapped = bool(NS) and "a" in stages and "U" in stages
    if "U" in stages and not u_overlapped:
        with ExitStack() as es:
            psU = [P.ps(es, "psU%d" % i, [128, 512], F32) for i in range(4)]
            for _ in u_gen(es, psU, 4):
                pass
            P.barrier()

    def stageA(prompt):
      with ExitStack() as es:
          psM = [P.ps(es, "psM%d" % i, [128, 512], F32) for i in range(2)]
          psT = [P.ps(es, "psT%d" % i, [128, 512], F32) for i in range(2)]
          psO = [P.ps(es, "psO%d" % i, [128, 512], F32) for i in range(2)]
          psB = [P.ps(es, "psB%d" % i, [128, 1024], BF16) for i in range(2)]
          rr = {"M": 0, "T": 0, "O": 0, "B": 0}

          def nxt(lst, key):
              b = lst[rr[key] % len(lst)]
              rr[key] += 1
              return b

          win = P.sb(es, "win", [128, 8, 3080], BF16, dma=True)
          load_w_bf16(win, w_in, 3080)
          bfb = P.sb(es, "bfb", [128, 8], F32, dma=True)
          dma("sp", bfb.t[:], b_f.partition_broadcast(128), bfb.k, writes=[bfb.k])
          xt = P.sb(es, "xt", [128, D], F32, dma=True)
          xT = P.sb(es, "xT", [128, 8, 128], BF16)
          pj = P.sb(es, "pj", [128, 3080], F32, dma=True)
          lft = P.sb(es, "lft", [128, 8], F32, dma=True)
          lfw = P.sb(es, "lfw", [128, 8], F32)
          accs = P.sb(es, "accs", [128, 8], F32)
          rbt = P.sb(es, "rbt", [128, 8], F32)
          qTs = P.sb(es, "qTs", [128, 4, 128], BF16); qTf = P.sb(es, "qTf", [128, 4, 128], BF16)
          Ot = P.sb(es, "Ot", [128, D], F32, dma=True)
          et = P.sb(es, "et", [128, 512], F32)
          WT = [P.sb(es, "WT", [128, 4, 128], BF16) for _ in range(2)]
          PTt = [P.sb(es, "PTt", [128, 128], BF16) for _ in range(2)]
          ntot = P.sb(es, "ntot", [128, 1], F32)
          rden = P.sb(es, "rden", [128, 1], F32)
          KMAX = T if prompt else PAST + TS
          SPb = P.sb(es, "SPb", [128, KMAX], F32)
          ZSb = P.sb(es, "ZSb", [128, KMAX], F32)
          Db = P.sb(es, "Db", [128, KMAX + 1], F32)
          Wb = P.sb(es, "Wb", [128, KMAX], BF16)
          op("dve", lambda: V.memset(Db.t[:, 0:1], 0.0), writes=[Db.k])
          if prompt:
              kTs = P.sb(es, "kTs", [128, 4, T], BF16); kTf = P.sb(es, "kTf", [128, 4, T], BF16)
              vS = P.sb(es, "vS", [128, KT, 512], BF16); vF = P.sb(es, "vF", [128, KT, 8, 65], BF16)
              call = P.sb(es, "call", [128, KT, 8], F32)
              biasb = P.sb(es, "biasb", [128, KT, 8], F32)

          def project_tile(x_rows, outs, row0):
              dma("sp", xt.t[:], x_rows, xt.k, writes=[xt.k])
              for hf in range(2):
                  pb = nxt(psT, "T")
                  for i in range(4):
                      k = hf * 4 + i
                      op("pe", lambda pb=pb, i=i, k=k: PE.transpose(pb.t[:, i * 128:(i + 1) * 128], xt.t[:, k * 128:(k + 1) * 128], idf.t[:]),
                         reads=[xt.k, idf.k], writes=[pb.k])
                  evac(xT.t[:, hf * 4:(hf + 1) * 4, :], pb.t[:].rearrange("p (k e) -> p k e", e=128), reads=[pb.k], writes=[xT.k])
              c0 = 0
              while c0 < 3080:
                  c1 = min(3080, c0 + 512)
                  pb = nxt(psM, "M")
                  for k in range(8):
                      op("pe", lambda pb=pb, k=k, c0=c0, c1=c1: PE.matmul(pb.t[:, 0:c1 - c0], xT.t[:, k, :], win.t[:, k, c0:c1],
                                                                         start=(k == 0), stop=(k == 7)),
                         reads=[xT.k, win.k], writes=[pb.k])
                  evac(pj.t[:, c0:c1], pb.t[:, 0:c1 - c0], reads=[pb.k], writes=[pj.k])
                  c0 = c1
              o_k1, o_v1, o_k2, o_v2, o_lf = outs
              rs_ = slice(row0, row0 + 128)
              dma("sp", o_k1[rs_, :], pj.t[:, 512:1024], t_out, reads=[pj.k])
              dma("sp", o_v1[rs_, :], pj.t[:, 1024:1536], t_out, reads=[pj.k])
              dma("sp", o_k2[rs_, :], pj.t[:, 2048:2560], t_out, reads=[pj.k])
              dma("sp", o_v2[rs_, :], pj.t[:, 2560:3072], t_out, reads=[pj.k])
              op("dve", lambda: V.tensor_add(lfw.t[:], pj.t[:, 3072:3080], bfb.t[:]), reads=[pj.k, bfb.k], writes=[lfw.k])
              op("act", lambda: A.activation(lfw.t[:], lfw.t[:], AF.Exp, scale=-1.0), reads=[lfw.k], writes=[lfw.k])
              op("act", lambda: A.activation(lfw.t[:], lfw.t[:], AF.Ln, bias=1.0), reads=[lfw.k], writes=[lfw.k])
              op("dve", lambda: V.tensor_scalar(lft.t[:], lfw.t[:], -1.0, None, ALU.mult), reads=[lfw.k], writes=[lft.k])
              dma("sp", o_lf[rs_, :], lft.t[:], t_out, reads=[lft.k])

          def transposes_qk(dst_q_s, dst_k_s, dst_q_f, dst_k_f):
              for (c0, dst) in ((0, dst_q_s), (512, dst_k_s), (1536, dst_q_f), (2048, dst_k_f)):
                  pb = nxt(psT, "T")
                  for i in range(4):
                      op("pe", lambda pb=pb, i=i, c0=c0: PE.transpose(pb.t[:, i * 128:(i + 1) * 128], pj.t[:, c0 + i * 128:c0 + (i + 1) * 128], idf.t[:]),
                         reads=[pj.k, idf.k], writes=[pb.k])
                  evac(dst[0], pb.t[:].rearrange("p (k e) -> p k e", e=128), reads=[pb.k], writes=[dst[1]])

          def attend_sb(S, nq, klen, qT_ap, q_tok, kT_ap, k_tok, v_of, v_tok, O_ap):
              et, SPb, ZSb, Db, Wb, ntot, WT, po = S["et"], S["SPb"], S["ZSb"], S["Db"], S["Wb"], S["ntot"], S["WT"], S["po"]
              c0 = 0
              while c0 < klen:
                  w = min(512, klen - c0)
                  pb = nxt(psM, "M")
                  op("pe", lambda pb=pb, c0=c0, w=w: PE.matmul(pb.t[0:nq, 0:w], qT_ap, kT_ap[:, c0:c0 + w], start=True, stop=True),
                     reads=[q_tok, k_tok], writes=[pb.k])
                  op("act", lambda pb=pb, w=w: A.activation(et.t[0:nq, 0:w], pb.t[0:nq, 0:w], AF.Exp, scale=scale), reads=[pb.k], writes=[et.k])
                  op("dve", lambda pb=pb, c0=c0, w=w: V.tensor_scalar(ZSb.t[0:nq, c0:c0 + w], pb.t[0:nq, 0:w], scale, None, ALU.mult),
                     reads=[pb.k], writes=[ZSb.k])
                  op("act", lambda c0=c0, w=w: A.activation(SPb.t[0:nq, c0:c0 + w], et.t[0:nq, 0:w], AF.Ln, bias=1.0), reads=[et.k], writes=[SPb.k])
                  c0 += w
                  yield
              d0 = klen - nq
              op("dve", lambda: V.tensor_mul(SPb.t[0:nq, d0:klen], SPb.t[0:nq, d0:klen], m_lt_f.t[0:nq, 0:nq]),
                 reads=[SPb.k, m_lt_f.k], writes=[SPb.k])
              op("dve", lambda: V.tensor_tensor_scan(Db.t[0:nq, 1:klen + 1], SPb.t[0:nq, 0:klen], SPb.t[0:nq, 0:klen], 0.0, ALU.add, ALU.max),
                 reads=[SPb.k], writes=[Db.k])
              yield
              op("dve", lambda: V.tensor_scalar(ntot.t[0:nq, :], Db.t[0:nq, klen:klen + 1], -1.0, None, ALU.mult), reads=[Db.k], writes=[ntot.k])
              op("dve", lambda: V.tensor_add(ZSb.t[0:nq, 0:klen], ZSb.t[0:nq, 0:klen], Db.t[0:nq, 0:klen]), reads=[ZSb.k, Db.k], writes=[ZSb.k])
              yield
              op("act", lambda: A.activation(Wb.t[0:nq, 0:klen], ZSb.t[0:nq, 0:klen], AF.Exp, bias=ntot.t[0:nq, :]),
                 reads=[ZSb.k, ntot.k], writes=[Wb.k])
              yield
              op("dve", lambda: V.tensor_mul(Wb.t[0:nq, d0:klen], Wb.t[0:nq, d0:klen], m_lt_b.t[0:nq, 0:nq]),
                 reads=[Wb.k, m_lt_b.k], writes=[Wb.k])
              nkt = (klen + 127) // 128
              a = 0
              gi = 0
              while a < nkt:
                  na = min(4, nkt - a)
                  pb = nxt(psB, "B"); wt = WT[gi % 2]; gi += 1
                  for i in range(na):
                      ks = min(128, klen - (a + i) * 128)
                      op("pe", lambda pb=pb, i=i, a=a, ks=ks: PE.transpose(pb.t[0:ks, i * 128:i * 128 + nq], Wb.t[0:nq, (a + i) * 128:(a + i) * 128 + ks], idb.t[0:nq, 0:nq]),
                         reads=[Wb.k, idb.k], writes=[pb.k])
                  ksl = min(128, klen - (a + na - 1) * 128)
                  if ksl == 128:
                      evac(wt.t[:, 0:na, 0:nq], pb.t[:, 0:na * 128].rearrange("p (k e) -> p k e", e=128)[:, :, 0:nq], reads=[pb.k], writes=[wt.k])
                  else:
                      if na > 1:
                          evac(wt.t[:, 0:na - 1, 0:nq], pb.t[:, 0:(na - 1) * 128].rearrange("p (k e) -> p k e", e=128)[:, :, 0:nq], reads=[pb.k], writes=[wt.k])
                      evac(wt.t[0:ksl, na - 1, 0:nq], pb.t[0:ksl, (na - 1) * 128:(na - 1) * 128 + nq], reads=[pb.k, wt.k], writes=[wt.k])
                  yield
                  for i in range(na):
                      ks = min(128, klen - (a + i) * 128)
                      op("pe", lambda wt=wt, i=i, a=a, ks=ks: PE.matmul(po.t[0:nq, 0:64], wt.t[0:ks, i, 0:nq], v_of(a + i, ks),
                                                                     start=(a + i == 0), stop=(a + i == nkt - 1)),
                         reads=[wt.k, v_tok], writes=[po.k])
                  a += na
              yield
              op("act", lambda: A.copy(O_ap, po.t[0:nq, 0:64]), reads=[po.k], writes=[Ot.k])

          def attend_fx(F, nq, klen, qT_ap, q_tok, kT_ap, k_tok, vaug_of, v_tok, bias_of, b_tok, O_ap):
              PTt, rden, po = F["PTt"], F["rden"], F["po"]
              nkt = (klen + 127) // 128
              for a in range(nkt):
                  ks = min(128, klen - a * 128)
                  pb = nxt(psM, "M"); pt = PTt[a % len(PTt)]
                  op("pe", lambda pb=pb, a=a, ks=ks: PE.matmul(pb.t[0:ks, 0:nq], kT_ap[:, a * 128:a * 128 + ks], qT_ap, start=True, stop=True),
                     reads=[q_tok, k_tok], writes=[pb.k])
                  op("act", lambda pb=pb, pt=pt, a=a, ks=ks: A.activation(pt.t[0:ks, 0:nq], pb.t[0:ks, 0:nq], AF.Exp, bias=bias_of(a, ks), scale=scale),
                     reads=[pb.k, b_tok], writes=[pt.k])
                  if a == nkt - 1:
                      op("dve", lambda pt=pt, ks=ks: V.tensor_mul(pt.t[0:ks, 0:nq], pt.t[0:ks, 0:nq], m_ge_b.t[0:ks, 0:nq]),
                         reads=[pt.k, m_ge_b.k], writes=[pt.k])
                  yield
                  op("pe", lambda pt=pt, a=a, ks=ks: PE.matmul(po.t[0:nq, 0:65], pt.t[0:ks, 0:nq], vaug_of(a, ks), start=(a == 0), stop=(a == nkt - 1)),
                     reads=[pt.k, v_tok], writes=[po.k])
              yield
              op("dve", lambda: V.reciprocal(rden.t[0:nq, :], po.t[0:nq, 64:65]), reads=[po.k], writes=[rden.k])
              op("dve", lambda: V.tensor_scalar(O_ap, po.t[0:nq, 0:64], rden.t[0:nq, :], None, ALU.mult), reads=[po.k, rden.k], writes=[Ot.k])

          def chain(gens):
              for g_ in gens:
                  yield from g_

          bg = [None]

          def run_lanes(lanes):
              lanes = [l for l in lanes]
              while lanes:
                  for l in list(lanes):
                      try:
                          next(l)
                      except StopIteration:
                          lanes.remove(l)
                  if bg[0] is not None:
                      next(bg[0], None)

          nset = 2 if prompt else 1
          sbsets = []
          for i_ in range(nset):
              if i_ == 0:
                  S_ = dict(et=et, SPb=SPb, ZSb=ZSb, Db=Db, Wb=Wb, ntot=ntot, WT=WT, po=psO[0])
              else:
                  S_ = dict(et=P.sb(es, "et1", [128, 512], F32), SPb=P.sb(es, "SPb1", [128, KMAX], F32), ZSb=P.sb(es, "ZSb1", [128, KMAX], F32),
                            Db=P.sb(es, "Db1", [128, KMAX + 1], F32), Wb=P.sb(es, "Wb1", [128, KMAX], BF16), ntot=P.sb(es, "ntot1", [128, 1], F32),
                            WT=[P.sb(es, "WT1", [128, 4, 128], BF16) for _ in range(2)], po=psO[1])
                  op("dve", lambda S_=S_: V.memset(S_["Db"].t[:, 0:1], 0.0), writes=[S_["Db"].k])
              sbsets.append(S_)
          PT4 = PTt + [P.sb(es, "PTt2", [128, 128], BF16) for _ in range(2)]
          fxset = dict(PTt=PT4, rden=rden, po=psT[0] if prompt else psO[1])
          fxset2 = dict(PTt=[P.sb(es, "PTt3", [128, 128], BF16) for _ in range(4)], rden=P.sb(es, "rden2", [128, 1], F32), po=psT[1]) if prompt else None

          outs_p = (o_sbk_p, o_sbv_p, o_fxk_p, o_fxv_p, o_lf_p)
          for sq in range(NP if prompt else 0):
              op("dve", lambda: V.memset(accs.t[:], 0.0), writes=[accs.k])
              op("pool", lambda: G.memset(vF.t[:, :, :, 64:65], 1.0), writes=[vF.k])
              for j in range(KT):
                  row0 = sq * T + j * 128
                  project_tile(x_p[sq, j * 128:(j + 1) * 128, :], outs_p, row0)
                  transposes_qk((qTs.t[:], qTs.k), (kTs.t[:, :, j * 128:(j + 1) * 128], kTs.k),
                                (qTf.t[:], qTf.k), (kTf.t[:, :, j * 128:(j + 1) * 128], kTf.k))
                  op("act", lambda j=j: A.copy(vS.t[:, j, :], pj.t[:, 1024:1536]), reads=[pj.k], writes=[vS.k])
                  op("dve", lambda j=j: V.tensor_copy(vF.t[:, j, :, 0:64], pj.t[:, 2560:3072].rearrange("p (h d) -> p h d", d=64)),
                     reads=[pj.k], writes=[vF.k])
                  pb = nxt(psM, "M")
                  op("pe", lambda pb=pb: PE.matmul(pb.t[:, 0:8], m_ge_f.t[:], lft.t[:], start=True, stop=False), reads=[m_ge_f.k, lft.k], writes=[pb.k])
                  op("pe", lambda pb=pb: PE.matmul(pb.t[:, 0:8], ones_f.t[:], accs.t[:], start=False, stop=True), reads=[ones_f.k, accs.k], writes=[pb.k])
                  op("pe", lambda pb=pb: PE.matmul(pb.t[:, 8:16], ones_f.t[:], accs.t[:], start=True, stop=True), reads=[ones_f.k, accs.k], writes=[pb.k])
                  op("dve", lambda pb=pb, j=j: V.tensor_copy(call.t[:, j, :], pb.t[:, 0:8]), reads=[pb.k], writes=[call.k])
                  op("dve", lambda pb=pb: V.tensor_copy(rbt.t[:], pb.t[:, 8:16]), reads=[pb.k], writes=[rbt.k])
                  op("dve", lambda: V.tensor_add(accs.t[:], accs.t[:], lft.t[:]), reads=[accs.k, lft.k], writes=[accs.k])
                  op("dve", lambda j=j: V.tensor_tensor(biasb.t[:, 0:j + 1, :], rbt.t[:].unsqueeze(1).to_broadcast([128, j + 1, 8]),
                                                        call.t[:, 0:j + 1, :], ALU.subtract),
                     reads=[rbt.k, call.k], writes=[biasb.k])
                  klen = (j + 1) * 128
                  def sb_unit(S_, h):
                      p_, b0 = h // 2, 64 * (h % 2)
                      return attend_sb(S_, 128, klen, qTs.t[b0:b0 + 64, p_, :], qTs.k, kTs.t[b0:b0 + 64, p_, 0:klen], kTs.k,
                                       lambda a, ks, h=h: vS.t[0:ks, a, h * 64:(h + 1) * 64], vS.k, Ot.t[:, h * 64:(h + 1) * 64])

                  def fx_unit(h, fxs):
                      p_, b0 = h // 2, 64 * (h % 2)
                      return attend_fx(fxs, 128, klen, qTf.t[b0:b0 + 64, p_, :], qTf.k, kTf.t[b0:b0 + 64, p_, 0:klen], kTf.k,
                                       lambda a, ks, h=h: vF.t[0:ks, a, h, :], vF.k,
                                       lambda a, ks, h=h: biasb.t[0:ks, a, h:h + 1], biasb.k, Ot.t[:, 512 + h * 64:512 + (h + 1) * 64])

                  run_lanes([chain([sb_unit(sbsets[0], h) for h in (0, 2, 4, 6)]),
                             chain([sb_unit(sbsets[1], h) for h in (1, 3, 5, 7)]),
                             chain([fx_unit(h, fxset) for h in (0, 2, 4, 6)]),
                             chain([fx_unit(h, fxset2) for h in (1, 3, 5, 7)])])
                  dma("sp", O_s[row0:row0 + 128, :], Ot.t[:], Ot.k, reads=[Ot.k], writes=[t_Os])
          P.barrier()

          if not prompt:
              es2 = ExitStack()
              with es2:
                  KL = PAST + TS
                  kst = P.sb(es2, "kst", [128, PT_, 128], F32, dma=True)
                  kTp = P.sb(es2, "kTp", [128, KL], BF16)
                  vp = P.sb(es2, "vp", [128, PT_ + 1, 128], BF16, dma=True)
                  vpa = P.sb(es2, "vpa", [128, PT_ + 1, 2, 65], BF16)
                  vnew = P.sb(es2, "vnew", [128, 1024], BF16, dma=True)
                  lfp = P.sb(es2, "lfp", [128, PT_, 8], F32, dma=True)
                  lfn = P.sb(es2, "lfn", [128, 8], F32, dma=True)
                  cal2 = P.sb(es2, "cal2", [128, PT_ + 1, 8], F32)
                  tot2 = P.sb(es2, "tot2", [128, PT_ + 1, 8], F32)
                  bia2 = P.sb(es2, "bia2", [128, PT_ + 1, 8], F32)
                  kTn_s = P.sb(es2, "kTn_s", [128, 4, 128], BF16); kTn_f = P.sb(es2, "kTn_f", [128, 4, 128], BF16)
                  row0 = NP * T
                  if u_overlapped:
                      bg[0] = u_gen(es2, psT, 1)
                  op("pool", lambda: G.memset(vp.t[:, PT_, :], 0.0), writes=[vp.k])
                  project_tile(x_s[:, :], (o_sbk_s, o_sbv_s, o_fxk_s, o_fxv_s, o_lf_s), 0)
                  transposes_qk((qTs.t[:], qTs.k), (kTn_s.t[:], kTn_s.k), (qTf.t[:], qTf.k), (kTn_f.t[:], kTn_f.k))
                  op("act", lambda: A.copy(vnew.t[:, 0:512], pj.t[:, 1024:1536]), reads=[pj.k], writes=[vnew.k])
                  op("dve", lambda: V.tensor_copy(vnew.t[:, 512:1024], pj.t[:, 2560:3072]), reads=[pj.k], writes=[vnew.k])
                  op("pool", lambda: G.memset(vpa.t[:, :, :, 64:65], 1.0), writes=[vpa.k])
                  for si in range(NS):
                      q0 = si * TS
                      P.mark("sample%d_start" % si)
                      for a_ in range(0, PT_, 4):
                          a1_ = min(PT_, a_ + 4)
                          dma("sp", lfp.t[:, a_:a1_, :], c_lf[si][a_ * 128:a1_ * 128, :].rearrange("(a s) h -> s a h", s=128), lfp.k,
                              writes=[lfp.k] if a_ == 0 else [], allow_slow_non_contiguous=True)
                      dma("sp", lfn.t[0:TS, :], lft.t[q0:q0 + TS, :], lfn.k, reads=[lft.k], writes=[lfn.k])
                      pb = nxt(psM, "M")
                      op("pe", lambda pb=pb: PE.matmul(pb.t[:, 0:PT_ * 8], m_ge_f.t[:], lfp.t[:].rearrange("p a h -> p (a h)"), start=True, stop=True),
                         reads=[m_ge_f.k, lfp.k], writes=[pb.k])
                      pb2 = nxt(psM, "M")
                      op("pe", lambda pb2=pb2: PE.matmul(pb2.t[:, 0:PT_ * 8], ones_f.t[:], lfp.t[:].rearrange("p a h -> p (a h)"), start=True, stop=True),
                         reads=[ones_f.k, lfp.k], writes=[pb2.k])
                      op("dve", lambda: V.memset(tot2.t[:, 0, :], 0.0), writes=[tot2.k])
                      op("dve", lambda pb2=pb2: V.tensor_copy(cal2.t[:, 0:PT_, :], pb2.t[:, 0:PT_ * 8].rearrange("p (a h) -> p a h", h=8)),
                         reads=[pb2.k], writes=[cal2.k])
                      for h in range(8):
                          op("dve", lambda h=h: V.tensor_tensor_scan(tot2.t[:, 1:PT_ + 1, h], cal2.t[:, 0:PT_, h], cal2.t[:, 0:PT_, h], 0.0, ALU.add, ALU.min),
                             reads=[cal2.k], writes=[tot2.k])
                      op("dve", lambda pb=pb: V.tensor_tensor(cal2.t[:, 0:PT_, :], pb.t[:, 0:PT_ * 8].rearrange("p (a h) -> p a h", h=8),
                                                            tot2.t[:, 0:PT_, :], ALU.add), reads=[pb.k, tot2.k], writes=[cal2.k])
                      pb3 = nxt(psM, "M")
                      op("pe", lambda pb3=pb3: PE.matmul(pb3.t[0:TS, 0:8], m_ge_f.t[0:TS, 0:TS], lfn.t[0:TS, :], start=True, stop=True),
                         reads=[m_ge_f.k, lfn.k], writes=[pb3.k])
                      op("dve", lambda pb3=pb3: V.tensor_tensor(cal2.t[0:TS, PT_, :], pb3.t[0:TS, 0:8], tot2.t[0:TS, PT_, :], ALU.add),
                         reads=[pb3.k, tot2.k], writes=[cal2.k])
                      op("dve", lambda: V.tensor_tensor(bia2.t[:, 0:PT_, :], tot2.t[:, PT_:PT_ + 1, :].to_broadcast([128, PT_, 8]), cal2.t[:, 0:PT_, :], ALU.subtract),
                         reads=[tot2.k, cal2.k], writes=[bia2.k])
                      op("dve", lambda: V.tensor_tensor(bia2.t[0:TS, PT_, :], tot2.t[0:TS, PT_, :], cal2.t[0:TS, PT_, :], ALU.subtract),
                         reads=[tot2.k, cal2.k], writes=[bia2.k])
                      P.mark("bias_done")
                      for grp in range(2):
                          ck = (c_sbk, c_fxk)[grp]; cv = (c_sbv, c_fxv)[grp]
                          kTn = (kTn_s, kTn_f)[grp]
                          for p_ in range(4):
                              cs = slice(p_ * 128, (p_ + 1) * 128)
                              for a_ in range(0, PT_, 8):
                                  a1_ = min(PT_, a_ + 8)
                                  dma("sp", kst.t[:, a_:a1_, :], ck[si][a_ * 128:a1_ * 128, cs].rearrange("(a s) c -> s a c", s=128), kst.k,
                                      writes=[kst.k] if a_ == 0 else [])
                              for a_ in range(0, PT_, 8):
                                  a1_ = min(PT_, a_ + 8)
                                  dma("pool", vp.t[:, a_:a1_, :], cv[si][a_ * 128:a1_ * 128, cs].rearrange("(a s) c -> s a c", s=128), vp.k,
                                      writes=[vp.k] if a_ == 0 else [])
                              dma("sp", vp.t[0:TS, PT_, :], vnew.t[q0:q0 + TS, grp * 512 + p_ * 128:grp * 512 + (p_ + 1) * 128], vp.k,
                                  reads=[vnew.k], writes=[vp.k])
                              a = 0
                              while a < PT_:
                                  pb = nxt(psT, "T")
                                  na = min(4, PT_ - a)
                                  for i in range(na):
                                      op("pe", lambda pb=pb, i=i, a=a: PE.transpose(pb.t[:, i * 128:(i + 1) * 128], kst.t[:, a + i, :], idf.t[:]),
                                         reads=[kst.k, idf.k], writes=[pb.k])
                                  evac(kTp.t[:, a * 128:(a + na) * 128], pb.t[:, 0:na * 128], reads=[pb.k], writes=[kTp.k])
                                  a += na
                              op("dve", lambda p_=p_, kTn=kTn: V.tensor_copy(kTp.t[:, PAST:PAST + TS], kTn.t[:, p_, q0:q0 + TS]), reads=[kTn.k], writes=[kTp.k])
                              if grp == 1:
                                  op("dve", lambda: V.tensor_copy(vpa.t[:, :, :, 0:64], vp.t[:].rearrange("p a (h d) -> p a h d", d=64)),
                                     reads=[vp.k], writes=[vpa.k])
                              P.mark("kv_loaded")
                              for hh in range(2):
                                  h = p_ * 2 + hh
                                  b0 = 64 * hh
                                  P.mark("head")
                                  if grp == 0:
                                      run_lanes([attend_sb(sbsets[0], TS, KL, qTs.t[b0:b0 + 64, p_, q0:q0 + TS], qTs.k, kTp.t[b0:b0 + 64, 0:KL], kTp.k,
                                                lambda a, ks, hh=hh: vp.t[0:ks, a, hh * 64:(hh + 1) * 64], vp.k, Ot.t[0:TS, h * 64:(h + 1) * 64])])
                                  else:
                                      run_lanes([attend_fx(fxset, TS, KL, qTf.t[b0:b0 + 64, p_, q0:q0 + TS], qTf.k, kTp.t[b0:b0 + 64, 0:KL], kTp.k,
                                                lambda a, ks, hh=hh: vpa.t[0:ks, a, hh, :], vpa.k,
                                                lambda a, ks, h=h: bia2.t[0:ks, a, h:h + 1], bia2.k, Ot.t[0:TS, 512 + h * 64:512 + (h + 1) * 64])])
                      dma("sp", O_s[row0 + q0:row0 + q0 + TS, :], Ot.t[0:TS, :], Ot.k, reads=[Ot.k], writes=[t_Os])
                  if bg[0] is not None:
                      for _ in bg[0]:
                          pass
                      bg[0] = None
          P.barrier()

    if NP and "A" in stages:
        stageA(True)
    if NS and "a" in stages:
        stageA(False)

    with ExitStack() as es:
      if "M" in stages:
            psM = [P.ps(es, "psM%d" % i, [128, 512], F32) for i in range(4)]
            psO = [P.ps(es, "psO%d" % i, [128, 512], F32) for i in range(2)]
            psB = [P.ps(es, "psB%d" % i, [128, 1024], BF16) for i in range(2)]
            rr = {"M": 0, "O": 0, "B": 0}

            def nxt(lst, key):
                b = lst[rr[key] % len(lst)]
                rr[key] += 1
                return b

            wo = P.sb(es, "wo", [128, 8, D], BF16, dma=True); wq = P.sb(es, "wq", [128, 8, D], BF16, dma=True)
            wmo = P.sb(es, "wmo", [128, 8, D], BF16, dma=True)
            wk = P.sb(es, "wk", [128, 8, D], BF16, dma=True); wv = P.sb(es, "wv", [128, 8, D], BF16, dma=True)
            gnc = P.sb(es, "gnc", [128, 8], F32)
            load_w_bf16(wo, w_out, D); load_w_bf16(wq, w_mq, D); load_w_bf16(wmo, w_mo, D)
            if NP:
                load_w_bf16(wk, w_mk, D); load_w_bf16(wv, w_mv, D)
            gnr = P.sb(es, "gnr", [8, 128], F32, dma=True)
            dma("sp", gnr.t[:], w_gn.rearrange("(k p) o -> k (p o)", p=128), gnr.k, writes=[gnr.k])
            op("pe", lambda: PE.transpose(psM[0].t[:, 0:8], gnr.t[:], idf.t[0:8, 0:8]), reads=[gnr.k, idf.k], writes=[psM[0].k])
            op("dve", lambda: V.tensor_copy(gnc.t[:], psM[0].t[:, 0:8]), reads=[psM[0].k], writes=[gnc.k])
            for k in range(8):
                op("dve", lambda k=k: V.tensor_scalar(wo.t[:, k, :], wo.t[:, k, :], gnc.t[:, k:k + 1], None, ALU.mult), reads=[wo.k, gnc.k], writes=[wo.k])
            lnp = {}
            for nm, src in (("g1", ln1_g), ("b1", ln1_b), ("g2", ln2_g), ("b2", ln2_b)):
                lnp[nm] = P.sb(es, "ln" + nm, [128, D], F32, dma=True)
                dma("sp", lnp[nm].t[:], src.partition_broadcast(128), lnp[nm].k, writes=[lnp[nm].k])
            memt = P.sb(es, "memt", [128, 2, D], F32, dma=True)
            memv = P.sb(es, "memv", [128, 2, D], F32, dma=True)
            memT = P.sb(es, "memT", [128, 8, NMEM], BF16)
            mkT = P.sb(es, "mkT", [128, 8, NMEM], BF16)
            mva = P.sb(es, "mva", [128, 2, 4, 257], BF16)
            op("pool", lambda: G.memset(mva.t[:, :, :, 256:257], 1.0), writes=[mva.k])
            NLANE = 2
            LB = []
            for L_ in range(NLANE):
                b_ = dict(
                    Oi=P.sb(es, "Oi", [128, D], F32, dma=True), xi=P.sb(es, "xi", [128, D], F32, dma=True),
                    sq2=P.sb(es, "sq2", [128, 512], F32), ssq=P.sb(es, "ssq", [128, 2], F32),
                    On=P.sb(es, "On", [128, D], BF16), OnT=P.sb(es, "OnT", [128, 8, 128], BF16),
                    r1=P.sb(es, "r1", [128, D], F32), x1=P.sb(es, "x1", [128, D], F32),
                    x1b=P.sb(es, "x1b", [128, D], BF16), x1T=P.sb(es, "x1T", [128, 8, 128], BF16),
                    qmT=P.sb(es, "qmT", [128, 8, 128], BF16),
                    PTm=[P.sb(es, "PTm", [128, 128], BF16) for _ in range(2)],
                    Om=P.sb(es, "Om", [128, D], BF16), OmT=P.sb(es, "OmT", [128, 8, 128], BF16),
                    rden=P.sb(es, "rdenm", [128, 1], F32),
                    x2=P.sb(es, "x2", [128, D], F32, dma=True),
                    st=P.sb(es, "lst", [128, 2, 6], F32), mv=P.sb(es, "lmv", [128, 2], F32), rs=P.sb(es, "lrs", [128, 1], F32),
                    pm=[psM[2 * L_], psM[2 * L_ + 1]], po=psO[L_], pb=psB[L_], pmi=[0])
                LB.append(b_)
            mscale = 256 ** -0.5

            def transpose_bf(src, dst):
                pb = nxt(psB, "B")
                for k in range(8):
                    op("pe", lambda pb=pb, k=k: PE.transpose(pb.t[:, k * 128:(k + 1) * 128], src.t[:, k * 128:(k + 1) * 128], idb.t[:]),
                       reads=[src.k, idb.k], writes=[pb.k])
                evac(dst.t[:], pb.t[:].rearrange("p (k e) -> p k e", e=128), reads=[pb.k], writes=[dst.k])

            def prep_mem(si_prompt, si_sample):
                if si_prompt is not None:
                    sq = si_prompt
                    dma("sp", memt.t[:], mem_p[sq].rearrange("(a m) d -> m a d", m=128), memt.k, writes=[memt.k])
                    for a in range(2):
                        for hf in range(2):
                            pb = nxt(psM, "M")
                            for i in range(4):
                                k = hf * 4 + i
                                op("pe", lambda pb=pb, i=i, k=k, a=a: PE.transpose(pb.t[:, i * 128:(i + 1) * 128], memt.t[:, a, k * 128:(k + 1) * 128], idf.t[:]),
                                   reads=[memt.k, idf.k], writes=[pb.k])
                            evac(memT.t[:, hf * 4:(hf + 1) * 4, a * 128:(a + 1) * 128], pb.t[:].rearrange("p (k e) -> p k e", e=128), reads=[pb.k], writes=[memT.k])
                    for (wt_, dstb, outd) in ((wk, memt, o_mk_p), (wv, memv, o_mv_p)):
                        for a in range(2):
                            for c in range(2):
                                pb = nxt(psM, "M")
                                for k in range(8):
                                    op("pe", lambda pb=pb, k=k, a=a, c=c, wt_=wt_: PE.matmul(pb.t[:], memT.t[:, k, a * 128:(a + 1) * 128], wt_.t[:, k, c * 512:(c + 1) * 512],
                                                                                         start=(k == 0), stop=(k == 7)),
                                       reads=[memT.k, wt_.k], writes=[pb.k])
                                evac(dstb.t[:, a, c * 512:(c + 1) * 512], pb.t[:], reads=[pb.k], writes=[dstb.k])
                        dma("sp", outd[sq * NMEM:(sq + 1) * NMEM, :].rearrange("(a m) d -> m a d", m=128), dstb.t[:], t_out, reads=[dstb.k])
                else:
                    dma("sp", memt.t[:], c_mk[si_sample].rearrange("(a m) d -> m a d", m=128), memt.k, writes=[memt.k])
                    dma("sp", memv.t[:], c_mv[si_sample].rearrange("(a m) d -> m a d", m=128), memv.k, writes=[memv.k])
                for a in range(2):
                    for hf in range(2):
                        pb = nxt(psM, "M")
                        for i in range(4):
                            k = hf * 4 + i
                            op("pe", lambda pb=pb, i=i, k=k, a=a: PE.transpose(pb.t[:, i * 128:(i + 1) * 128], memt.t[:, a, k * 128:(k + 1) * 128], idf.t[:]),
                               reads=[memt.k, idf.k], writes=[pb.k])
                        evac(mkT.t[:, hf * 4:(hf + 1) * 4, a * 128:(a + 1) * 128], pb.t[:].rearrange("p (k e) -> p k e", e=128), reads=[pb.k], writes=[mkT.k])
                op("dve", lambda: V.tensor_copy(mva.t[:, :, :, 0:256], memv.t[:].rearrange("p a (h d) -> p a h d", d=256)), reads=[memv.k], writes=[mva.k])

            def ln_gen(B_, r, g_b, b_b, outb):
                st, mv, rs = B_["st"], B_["mv"], B_["rs"]
                for c in range(2):
                    op("dve", lambda c=c: V.bn_stats(st.t[:, c, :], r.t[:, c * 512:(c + 1) * 512]), reads=[r.k], writes=[st.k])
                yield
                op("dve", lambda: V.bn_aggr(mv.t[:], st.t[:]), reads=[st.k], writes=[mv.k])
                op("dve", lambda: V.tensor_scalar(rs.t[:], mv.t[:, 1:2], LN_EPS, None, ALU.add), reads=[mv.k], writes=[rs.k])
                yield
                op("act", lambda: A.sqrt(rs.t[:], rs.t[:]), reads=[rs.k], writes=[rs.k])
                yield
                op("dve", lambda: V.reciprocal(rs.t[:], rs.t[:]), reads=[rs.k], writes=[rs.k])
                yield
                op("dve", lambda: V.tensor_scalar(r.t[:], r.t[:], mv.t[:, 0:1], rs.t[:], ALU.subtract, ALU.mult),
                   reads=[r.k, mv.k, rs.k], writes=[r.k])
                yield
                op("dve", lambda: V.tensor_mul(r.t[:], r.t[:], g_b.t[:]), reads=[r.k, g_b.k], writes=[r.k])
                yield
                op("dve", lambda: V.tensor_add(outb.t[:], r.t[:], b_b.t[:]), reads=[r.k, b_b.k], writes=[outb.k])
                yield

            def pmn(B_):
                b = B_["pm"][B_["pmi"][0] % 2]
                B_["pmi"][0] += 1
                return b

            def tr_gen(B_, src, dst):
                pb = B_["pb"]
                for k in range(8):
                    op("pe", lambda k=k: PE.transpose(pb.t[:, k * 128:(k + 1) * 128], src.t[:, k * 128:(k + 1) * 128], idb.t[:]),
                       reads=[src.k, idb.k], writes=[pb.k])
                yield
                op("act", lambda: A.copy(dst.t[:], pb.t[:].rearrange("p (k e) -> p k e", e=128)), reads=[pb.k], writes=[dst.k])
                yield

            def front_gen(B_, row0, x_rows):
                Oi, xi, sq2, ssq, On, OnT, r1, x1, x1b, x1T, qmT = (B_[k] for k in ("Oi", "xi", "sq2", "ssq", "On", "OnT", "r1", "x1", "x1b", "x1T", "qmT"))
                dma("sp", Oi.t[:], O_s[row0:row0 + 128, :], Oi.k, reads=[t_Os], writes=[Oi.k])
                dma("sp", xi.t[:], x_rows, xi.k, writes=[xi.k])
                yield
                for g_ in range(2):
                    op("act", lambda g_=g_: A.activation(sq2.t[:], Oi.t[:, g_ * 512:(g_ + 1) * 512], AF.Square, accum_out=ssq.t[:, g_:g_ + 1]),
                       reads=[Oi.k], writes=[sq2.k, ssq.k])
                yield
                op("dve", lambda: V.tensor_scalar(ssq.t[:], ssq.t[:], 1.0 / 512, GN_EPS, ALU.mult, ALU.add), reads=[ssq.k], writes=[ssq.k])
                yield
                op("act", lambda: A.sqrt(ssq.t[:], ssq.t[:]), reads=[ssq.k], writes=[ssq.k])
                yield
                op("dve", lambda: V.reciprocal(ssq.t[:], ssq.t[:]), reads=[ssq.k], writes=[ssq.k])
                yield
                for g_ in range(2):
                    op("dve", lambda g_=g_: V.tensor_scalar(On.t[:, g_ * 512:(g_ + 1) * 512], Oi.t[:, g_ * 512:(g_ + 1) * 512], ssq.t[:, g_:g_ + 1], None, ALU.mult),
                       reads=[Oi.k, ssq.k], writes=[On.k])
                yield
                yield from tr_gen(B_, On, OnT)
                for c in range(2):
                    pb = pmn(B_)
                    for k in range(8):
                        op("pe", lambda pb=pb, k=k, c=c: PE.matmul(pb.t[:], OnT.t[:, k, :], wo.t[:, k, c * 512:(c + 1) * 512], start=(k == 0), stop=(k == 7)),
                           reads=[OnT.k, wo.k], writes=[pb.k])
                    yield
                    op("dve", lambda pb=pb, c=c: V.scalar_tensor_tensor(r1.t[:, c * 512:(c + 1) * 512], xi.t[:, c * 512:(c + 1) * 512], ALPHA, pb.t[:], ALU.mult, ALU.add),
                       reads=[xi.k, pb.k], writes=[r1.k])
                yield
                yield from ln_gen(B_, r1, lnp["g1"], lnp["b1"], x1)
                op("act", lambda: A.copy(x1b.t[:], x1.t[:]), reads=[x1.k], writes=[x1b.k])
                yield
                yield from tr_gen(B_, x1b, x1T)
                for hf in range(2):
                    pb = pmn(B_)
                    for i in range(4):
                        c = hf * 4 + i
                        for k in range(8):
                            op("pe", lambda pb=pb, i=i, c=c, k=k: PE.matmul(pb.t[:, i * 128:(i + 1) * 128], wq.t[:, k, c * 128:(c + 1) * 128], x1T.t[:, k, :],
                                                                           start=(k == 0), stop=(k == 7)),
                               reads=[wq.k, x1T.k], writes=[pb.k])
                    yield
                    op("act", lambda pb=pb, hf=hf: A.copy(qmT.t[:, hf * 4:(hf + 1) * 4, :], pb.t[:].rearrange("p (k e) -> p k e", e=128)), reads=[pb.k], writes=[qmT.k])
                yield

            def attn_gen(B_, q0, nq):
                qmT, PTm, Om, rden, po = B_["qmT"], B_["PTm"], B_["Om"], B_["rden"], B_["po"]
                for h in range(4):
                    for a in range(2):
                        pb = pmn(B_); pt = PTm[a]
                        for cc in range(2):
                            op("pe", lambda pb=pb, a=a, cc=cc, h=h: PE.matmul(pb.t[:, 0:nq], mkT.t[:, 2 * h + cc, a * 128:(a + 1) * 128], qmT.t[:, 2 * h + cc, q0:q0 + nq],
                                                                            start=(cc == 0), stop=(cc == 1)),
                               reads=[mkT.k, qmT.k], writes=[pb.k])
                        yield
                        op("act", lambda pb=pb, pt=pt: A.activation(pt.t[:, 0:nq], pb.t[:, 0:nq], AF.Exp, scale=mscale), reads=[pb.k], writes=[pt.k])
                        yield
                        op("pe", lambda pt=pt, a=a, h=h: PE.matmul(po.t[q0:q0 + nq, 0:257], pt.t[:, 0:nq], mva.t[:, a, h, :], start=(a == 0), stop=(a == 1)),
                           reads=[pt.k, mva.k], writes=[po.k])
                    yield
                    op("dve", lambda: V.reciprocal(rden.t[q0:q0 + nq, :], po.t[q0:q0 + nq, 256:257]), reads=[po.k], writes=[rden.k])
                    op("dve", lambda h=h: V.tensor_scalar(Om.t[q0:q0 + nq, h * 256:(h + 1) * 256], po.t[q0:q0 + nq, 0:256], rden.t[q0:q0 + nq, :], None, ALU.mult),
                       reads=[po.k, rden.k], writes=[Om.k])
                    yield

            def back_gen(B_, row0):
                Om, OmT, x1, x2, r2 = B_["Om"], B_["OmT"], B_["x1"], B_["x2"], B_["r1"]
                yield from tr_gen(B_, Om, OmT)
                for c in range(2):
                    pb = pmn(B_)
                    for k in range(8):
                        op("pe", lambda pb=pb, k=k, c=c: PE.matmul(pb.t[:], OmT.t[:, k, :], wmo.t[:, k, c * 512:(c + 1) * 512], start=(k == 0), stop=(k == 7)),
                           reads=[OmT.k, wmo.k], writes=[pb.k])
                    yield
                    op("dve", lambda pb=pb, c=c: V.scalar_tensor_tensor(r2.t[:, c * 512:(c + 1) * 512], x1.t[:, c * 512:(c + 1) * 512], ALPHA, pb.t[:], ALU.mult, ALU.add),
                       reads=[x1.k, pb.k], writes=[r2.k])
                yield
                yield from ln_gen(B_, r2, lnp["g2"], lnp["b2"], x2)
                dma("sp", x2_s[row0:row0 + 128, :], x2.t[:], x2.k, reads=[x2.k], writes=[t_x2s])
                yield

            def tile_gen(B_, row0, x_rows):
                yield from front_gen(B_, row0, x_rows)
                yield from attn_gen(B_, 0, 128)
                yield from back_gen(B_, row0)

            def chain(gens):
                for g_ in gens:
                    yield from g_

            def run_lanes(lanes):
                lanes = list(lanes)
                while lanes:
                    for l in list(lanes):
                        try:
                            next(l)
                        except StopIteration:
                            lanes.remove(l)

            for sq in range(NP):
                prep_mem(sq, None)
                run_lanes([chain([tile_gen(LB[L_], sq * T + j * 128, x_p[sq, j * 128:(j + 1) * 128, :]) for j in range(L_, KT, NLANE)]) for L_ in range(NLANE)])
            if NS:
                row0 = NP * T
                for _ in front_gen(LB[0], row0, x_s[:, :]):
                    pass
                for si in range(NS):
                    prep_mem(None, si)
                    for _ in attn_gen(LB[0], si * TS, TS):
                        pass
                for _ in back_gen(LB[0], row0):
                    pass
            P.barrier()

    with ExitStack() as es:
      if "B" in stages:
            NTB = 2
            psA = [P.ps(es, "psA%d" % i, [128, 512], F32) for i in range(4)]
            psH = [P.ps(es, "psH%d" % i, [128, 512], F32) for i in range(2)]
            psG = [P.ps(es, "psG%d" % i, [128, 512], F32) for i in range(2)]
            rr = {"H": 0, "G": 0}

            def nxt(lst, key):
                b = lst[rr[key] % len(lst)]
                rr[key] += 1
                return b

            wpq = P.sb(es, "wpq", [128, 8, 2048], BF16, dma=True)
            load_w_bf16(wpq, w_pq, 2048)
            kT2 = P.sb(es, "kT2", [128, 16, 128], BF16)
            with ExitStack() as est:
                kst2 = P.sb(est, "kst2", [128, 16, 128], F32, dma=True)
                for hh_ in range(0, 8, 4):
                    dma("sp", kst2.t[:, hh_:hh_ + 4, :], keys_a[hh_:hh_ + 4].rearrange("h k d -> k h d"), kst2.k, writes=[kst2.k] if hh_ == 0 else [])
                for hh_ in range(0, 8, 4):
                    dma("sp", kst2.t[:, 8 + hh_:8 + hh_ + 4, :], keys_b[hh_:hh_ + 4].rearrange("h k d -> k h d"), kst2.k, writes=[])
                for q4 in range(4):
                    pb = nxt(psG, "G")
                    for i in range(4):
                        op("pe", lambda pb=pb, i=i, q4=q4: PE.transpose(pb.t[:, i * 128:(i + 1) * 128], kst2.t[:, q4 * 4 + i, :], idf.t[:]),
                           reads=[kst2.k, idf.k], writes=[pb.k])
                    evac(kT2.t[:, q4 * 4:(q4 + 1) * 4, :], pb.t[:].rearrange("p (k e) -> p k e", e=128), reads=[pb.k], writes=[kT2.k])
                P.barrier()
            g3 = P.sb(es, "g3", [128, D], F32, dma=True); b3 = P.sb(es, "b3", [128, D], F32, dma=True)
            dma("sp", g3.t[:], ln3_g.partition_broadcast(128), g3.k, writes=[g3.k])
            dma("sp", b3.t[:], ln3_b.partition_broadcast(128), b3.k, writes=[b3.k])
            NTMAX = NTB * 128
            x2tq = [[P.sb(es, "x2t", [128, D], F32, dma=True) for _ in range(NTB)] for _q in range(2)]
            x2Tq = [P.sb(es, "x2T", [128, 8, NTMAX], BF16) for _q in range(2)]
            qpT = P.sb(es, "qpT", [128, 16, 128], BF16)
            SC = P.sb(es, "SC", [128, 16, 128], F32)
            TMP = P.sb(es, "TMP", [128, 16, 128], F32)
            VAB = P.sb(es, "VAB", [128, 16, 16], F32)
            IAB = P.sb(es, "IAB", [128, 16, 16], U32)
            IABf = P.sb(es, "IABf", [128, 16, 16], F32)
            TOP = P.sb(es, "TOP", [128, 8, 16], F32)
            POS = P.sb(es, "POS", [128, 8, 16], U32)
            POSa = P.sb(es, "POSa", [128, 8, 16], U32); POSb = P.sb(es, "POSb", [128, 8, 16], U32)
            POSaf = P.sb(es, "POSaf", [128, 8, 16], F32); POSbf = P.sb(es, "POSbf", [128, 8, 16], F32)
            KAf = P.sb(es, "KAf", [128, 8, 16], F32); KBf = P.sb(es, "KBf", [128, 8, 16], F32)
            Gt = P.sb(es, "Gt", [128, 8, 16], F32)
            zsum = P.sb(es, "zsum", [128, 8], F32)
            KATq = [[P.sb(es, "KAT", [128, 128], BF16) for _ in range(NTB)] for _q in range(2)]
            KBTq = [[P.sb(es, "KBT", [128, 128], BF16) for _ in range(NTB)] for _q in range(2)]
            GTTq = [[P.sb(es, "GTT", [128, 128], BF16) for _ in range(NTB)] for _q in range(2)]
            TB = 8
            PA = [P.sb(es, "PA", [128, TB, 128], BF16) for _ in range(2)]
            PB = [P.sb(es, "PB", [128, TB, 128], BF16) for _ in range(2)]
            GTall = P.sb(es, "GTall", [128, NTMAX, 128], BF16)
            iota_b = P.sb(es, "iota_b", [128, 128], BF16)
            op("dve", lambda: V.tensor_copy(iota_b.t[:], iota_f.t[:]), reads=[iota_f.k], writes=[iota_b.k])
            NBK = 2
            NUB = 3
            ubuf = [P.sb(es, "ubuf", [128, NBK, 1024], BF16, dma=True) for _ in range(NUB)]
            vbuf = [P.sb(es, "vbuf", [128, NBK, 1024], BF16, dma=True) for _ in range(NUB)]
            NHB = 3
            hg = [P.sb(es, "hg", [128, NTMAX], BF16) for _ in range(NHB)]
            hgg = [P.sb(es, "hgg", [128, NTMAX], BF16) for _ in range(NHB)]
            r3 = Buf(SC.t[:].rearrange("p a k -> p (a k)")[:, 0:D], SC.k)
            candv = SC.t[:].rearrange("p (h a) k -> p h (a k)", a=2)
            tmpCv = TMP.t[:].rearrange("p (h a) k -> p h (a k)", a=2)
            eqwv = TMP.t[:].rearrange("p (h a) (r i) -> p h (a r) i", a=2, i=16)

            rbank = [psG[1]]

            def route_gen(q, ti, r0):
                x2tile = x2tq[q][ti]; x2T = x2Tq[q]
                KAT, KBT, GTT = KATq[q][ti], KBTq[q][ti], GTTq[q][ti]
                tsl = slice(ti * 128, (ti + 1) * 128)
                dma("sp", x2tile.t[:], x2_s[r0:r0 + 128, :], x2tile.k, reads=[t_x2s], writes=[x2tile.k])
                for hf in range(2):
                    pb = rbank[0]
                    for i in range(4):
                        k = hf * 4 + i
                        op("pe", lambda pb=pb, i=i, k=k: PE.transpose(pb.t[:, i * 128:(i + 1) * 128], x2tile.t[:, k * 128:(k + 1) * 128], idf.t[:]),
                           reads=[x2tile.k, idf.k], writes=[pb.k])
                    op("act", lambda pb=pb: A.copy(x2T.t[:, hf * 4:(hf + 1) * 4, tsl], pb.t[:].rearrange("p (k e) -> p k e", e=128)), reads=[pb.k], writes=[x2T.k])
                    yield
                for q4 in range(4):
                    pb = rbank[0]
                    for i in range(4):
                        c = q4 * 4 + i
                        for k in range(8):
                            op("pe", lambda pb=pb, i=i, c=c, k=k: PE.matmul(pb.t[:, i * 128:(i + 1) * 128], wpq.t[:, k, c * 128:(c + 1) * 128], x2T.t[:, k, tsl],
                                                                           start=(k == 0), stop=(k == 7)),
                               reads=[wpq.k, x2T.k], writes=[pb.k])
                    op("act", lambda pb=pb: A.copy(qpT.t[:, q4 * 4:(q4 + 1) * 4, :], pb.t[:].rearrange("p (k e) -> p k e", e=128)), reads=[pb.k], writes=[qpT.k])
                    yield
                for q4 in range(4):
                    pb = rbank[0]
                    for i in range(4):
                        s_ = q4 * 4 + i
                        half, h = s_ // 8, s_ % 8
                        op("pe", lambda pb=pb, i=i, s_=s_, half=half, h=h: PE.matmul(pb.t[:, i * 128:(i + 1) * 128], qpT.t[:, h * 2 + half, :], kT2.t[:, s_, :],
                                                                                     start=True, stop=True),
                           reads=[qpT.k, kT2.k], writes=[pb.k])
                    op("act", lambda pb=pb: A.copy(SC.t[:, q4 * 4:(q4 + 1) * 4, :], pb.t[:].rearrange("p (k e) -> p k e", e=128)), reads=[pb.k], writes=[SC.k])
                    yield
                for s_ in range(16):
                    op("dve", lambda s_=s_: V.max(VAB.t[:, s_, 0:8], SC.t[:, s_, :]), reads=[SC.k], writes=[VAB.k])
                    if s_ % 2 == 1:
                        yield
                for s_ in range(16):
                    op("dve", lambda s_=s_: V.match_replace(TMP.t[:, s_, :], VAB.t[:, s_, 0:8], SC.t[:, s_, :], -1e30), reads=[SC.k, VAB.k], writes=[TMP.k])
                    if s_ % 2 == 1:
                        yield
                for s_ in range(16):
                    op("dve", lambda s_=s_: V.max(VAB.t[:, s_, 8:16], TMP.t[:, s_, :]), reads=[TMP.k], writes=[VAB.k])
                    if s_ % 2 == 1:
                        yield
                for s_ in range(16):
                    op("dve", lambda s_=s_: V.max_index(IAB.t[:, s_, 0:8], VAB.t[:, s_, 0:8], SC.t[:, s_, :]), reads=[SC.k, VAB.k], writes=[IAB.k])
                    if s_ % 2 == 1:
                        yield
                for s_ in range(16):
                    op("dve", lambda s_=s_: V.max_index(IAB.t[:, s_, 8:16], VAB.t[:, s_, 8:16], TMP.t[:, s_, :]), reads=[TMP.k, VAB.k], writes=[IAB.k])
                    if s_ % 2 == 1:
                        yield
                op("dve", lambda: V.tensor_copy(IABf.t[:], IAB.t[:]), reads=[IAB.k], writes=[IABf.k])
                op("dve", lambda: V.tensor_tensor(candv.rearrange("p h (i j) -> p h i j", j=16),
                                                  VAB.t[:, 0:8, :].unsqueeze(3).to_broadcast([128, 8, 16, 16]),
                                                  VAB.t[:, 8:16, :].unsqueeze(2).to_broadcast([128, 8, 16, 16]), ALU.add),
                   reads=[VAB.k], writes=[SC.k])
                for h in range(8):
                    op("dve", lambda h=h: V.max(TOP.t[:, h, 0:8], candv[:, h, :]), reads=[SC.k], writes=[TOP.k])
                    if h % 2 == 1:
                        yield
                for h in range(8):
                    op("dve", lambda h=h: V.match_replace(tmpCv[:, h, :], TOP.t[:, h, 0:8], candv[:, h, :], -1e30), reads=[SC.k, TOP.k], writes=[TMP.k])
                    if h % 2 == 1:
                        yield
                for h in range(8):
                    op("dve", lambda h=h: V.max(TOP.t[:, h, 8:16], tmpCv[:, h, :]), reads=[TMP.k], writes=[TOP.k])
                    if h % 2 == 1:
                        yield
                for h in range(8):
                    op("dve", lambda h=h: V.max_index(POS.t[:, h, 0:8], TOP.t[:, h, 0:8], candv[:, h, :]), reads=[SC.k, TOP.k], writes=[POS.k])
                    if h % 2 == 1:
                        yield
                for h in range(8):
                    op("dve", lambda h=h: V.max_index(POS.t[:, h, 8:16], TOP.t[:, h, 8:16], tmpCv[:, h, :]), reads=[TMP.k, TOP.k], writes=[POS.k])
                    if h % 2 == 1:
                        yield
                op("dve", lambda: V.tensor_single_scalar(POSa.t[:], POS.t[:], 4, ALU.logical_shift_right), reads=[POS.k], writes=[POSa.k])
                op("dve", lambda: V.tensor_single_scalar(POSb.t[:], POS.t[:], 15, ALU.bitwise_and), reads=[POS.k], writes=[POSb.k])
                op("dve", lambda: V.tensor_copy(POSaf.t[:], POSa.t[:]), reads=[POSa.k], writes=[POSaf.k])
                op("dve", lambda: V.tensor_copy(POSbf.t[:], POSb.t[:]), reads=[POSb.k], writes=[POSbf.k])
                op("dve", lambda: V.tensor_tensor(Gt.t[:], TOP.t[:], TOP.t[:, :, 0:1].to_broadcast([128, 8, 16]), ALU.subtract), reads=[TOP.k], writes=[Gt.k])
                for (posf, half, dst) in ((POSaf, 0, KAf), (POSbf, 1, KBf)):
                    op("dve", lambda posf=posf: V.tensor_tensor(eqwv, iota_f.t[:, 0:16].unsqueeze(1).unsqueeze(1).to_broadcast([128, 8, 16, 16]),
                                                                posf.t[:].unsqueeze(3).to_broadcast([128, 8, 16, 16]), ALU.is_equal),
                       reads=[iota_f.k, posf.k], writes=[TMP.k])
                    op("dve", lambda half=half: V.tensor_tensor(eqwv, eqwv, IABf.t[:, half * 8:(half + 1) * 8, :].unsqueeze(2).to_broadcast([128, 8, 16, 16]), ALU.mult),
                       reads=[TMP.k, IABf.k], writes=[TMP.k])
                    op("dve", lambda dst=dst: V.tensor_reduce(dst.t[:], eqwv, AX.X, ALU.add), reads=[TMP.k], writes=[dst.k])
                for _ in range(6):
                    yield
                op("act", lambda: A.activation(Gt.t[:], Gt.t[:], AF.Exp), reads=[Gt.k], writes=[Gt.k])
                for _ in range(4):
                    yield
                op("dve", lambda: V.tensor_reduce(zsum.t[:], Gt.t[:], AX.X, ALU.add), reads=[Gt.k], writes=[zsum.k])
                op("dve", lambda: V.reciprocal(zsum.t[:], zsum.t[:]), reads=[zsum.k], writes=[zsum.k])
                op("dve", lambda: V.tensor_tensor(Gt.t[:], Gt.t[:], zsum.t[:].unsqueeze(2).to_broadcast([128, 8, 16]), ALU.mult), reads=[Gt.k, zsum.k], writes=[Gt.k])
                for _ in range(8):
                    yield
                pb = rbank[0]
                for i, src in enumerate((KAf, KBf, Gt)):
                    op("pe", lambda pb=pb, i=i, src=src: PE.transpose(pb.t[:, i * 128:(i + 1) * 128], src.t[:].rearrange("p h r -> p (h r)"), idf.t[:]),
                       reads=[src.k, idf.k], writes=[pb.k])
                op("act", lambda pb=pb: A.copy(KAT.t[:], pb.t[:, 0:128]), reads=[pb.k], writes=[KAT.k])
                op("act", lambda pb=pb: A.copy(KBT.t[:], pb.t[:, 128:256]), reads=[pb.k], writes=[KBT.k])
                op("act", lambda pb=pb: A.copy(GTT.t[:], pb.t[:, 256:384]), reads=[pb.k], writes=[GTT.k])
            gbanks = [psH[0], psH[1], psG[0], psG[1]]
            gbi = [0]

            def gbuild(q, ti):
                KAT, KBT, GTT = KATq[q][ti], KBTq[q][ti], GTTq[q][ti]
                iob = iota_b.t[:].unsqueeze(1).to_broadcast([128, TB, 128])

                def stage1(nb):
                    pa = PA[nb % 2]; pbt = PB[nb % 2]
                    n0 = nb * TB
                    op("dve", lambda: V.tensor_tensor(pa.t[:], iob, KAT.t[:, n0:n0 + TB].unsqueeze(2).to_broadcast([128, TB, 128]), ALU.is_equal),
                       reads=[iota_b.k, KAT.k], writes=[pa.k])
                    if nb % 3 == 2:
                        op("dve", lambda: V.tensor_tensor(pa.t[:], pa.t[:], GTT.t[:, n0:n0 + TB].unsqueeze(2).to_broadcast([128, TB, 128]), ALU.mult),
                           reads=[pa.k, GTT.k], writes=[pa.k])
                    else:
                        op("pool", lambda: G.tensor_tensor(pa.t[:], pa.t[:], GTT.t[:, n0:n0 + TB].unsqueeze(2).to_broadcast([128, TB, 128]), ALU.mult),
                           reads=[pa.k, GTT.k], writes=[pa.k])
                    op("dve", lambda: V.tensor_tensor(pbt.t[:], iob, KBT.t[:, n0:n0 + TB].unsqueeze(2).to_broadcast([128, TB, 128]), ALU.is_equal),
                       reads=[iota_b.k, KBT.k], writes=[pbt.k])

                def stage2(nb):
                    pa = PA[nb % 2]; pbt = PB[nb % 2]
                    n0 = nb * TB
                    for q_ in range(TB // 4):
                        pg = gbanks[gbi[0] % 4]; gbi[0] += 1
                        for i in range(4):
                            op("pe", lambda pg=pg, i=i, q_=q_: PE.matmul(pg.t[:, i * 128:(i + 1) * 128], pbt.t[:, q_ * 4 + i, :], pa.t[:, q_ * 4 + i, :], start=True, stop=True),
                               reads=[pa.k, pbt.k], writes=[pg.k])
                        c0 = ti * 128 + n0 + q_ * 4
                        op("act", lambda pg=pg, c0=c0: A.copy(GTall.t[:, c0:c0 + 4, :], pg.t[:].rearrange("p (n a) -> p n a", a=128)), reads=[pg.k], writes=[GTall.k])

                ngr = 128 // TB
                stage1(0)
                for nb in range(ngr):
                    if nb + 1 < ngr:
                        stage1(nb + 1)
                    stage2(nb)

            tile_rows = [(t * 128) for t in range(NTILE)]
            hbanks = [psH[0], psH[1], psG[0]]
            LA = 2

            def load_u(g):
                sl = g % NUB
                dma("sp", ubuf[sl].t[:], uT_s[g * NBK:(g + 1) * NBK].rearrange("b p f -> p b f"), ubuf[sl].k, reads=[t_uTs], writes=[ubuf[sl].k])

            def load_v(g):
                sl = g % NUB
                dma("sp", vbuf[sl].t[:], v_s[g * NBK * 128:(g + 1) * NBK * 128, :].rearrange("(b e) d -> e b d", e=128), vbuf[sl].k, reads=[t_vs], writes=[vbuf[sl].k])

            def epilogue_gen(pidx):
                p0, ntp = passes[pidx]
                st = lnw["st"]; mv = lnw["mv"]; rs = lnw["rs"]
                for ti in range(ntp):
                    r = x2tq[pidx % 2][ti]
                    r0 = tile_rows[p0 + ti]
                    for c in range(2):
                        op("dve", lambda c=c, r=r: V.bn_stats(st.t[:, c, :], r.t[:, c * 512:(c + 1) * 512]), reads=[r.k], writes=[st.k])
                    yield
                    op("dve", lambda: V.bn_aggr(mv.t[:], st.t[:]), reads=[st.k], writes=[mv.k])
                    op("dve", lambda: V.tensor_scalar(rs.t[:], mv.t[:, 1:2], LN_EPS, None, ALU.add), reads=[mv.k], writes=[rs.k])
                    yield
                    op("act", lambda: A.sqrt(rs.t[:], rs.t[:]), reads=[rs.k], writes=[rs.k])
                    yield
                    op("dve", lambda: V.reciprocal(rs.t[:], rs.t[:]), reads=[rs.k], writes=[rs.k])
                    yield
                    op("dve", lambda r=r: V.tensor_scalar(r.t[:], r.t[:], mv.t[:, 0:1], rs.t[:], ALU.subtract, ALU.mult),
                       reads=[r.k, mv.k, rs.k], writes=[r.k])
                    yield
                    op("dve", lambda r=r: V.tensor_mul(r.t[:], r.t[:], g3.t[:]), reads=[r.k, g3.k], writes=[r.k])
                    yield
                    op("dve", lambda r=r: V.tensor_add(r.t[:], r.t[:], b3.t[:]), reads=[r.k, b3.k], writes=[r.k])
                    yield
                    if r0 < NP * T:
                        dma("sp", y_p[r0:r0 + 128, :], r.t[:], r.k, reads=[r.k])
                    else:
                        dma("sp", y_s[r0 - NP * T:r0 - NP * T + 128, :], r.t[:], r.k, reads=[r.k])

            passes = []
            pi = 0
            while pi < NTILE:
                nt = min(NTB, NTILE - pi)
                passes.append((pi, nt))
                pi += nt

            def route_pass(pidx):
                p0, ntp = passes[pidx]
                for ti in range(ntp):
                    yield from route_gen(pidx % 2, ti, tile_rows[p0 + ti])

            for _ in route_pass(0):
                pass
            NG = 128 // NBK
            for pidx, (pi, nt) in enumerate(passes):
                q = pidx % 2
                NT = nt * 128
                x2T = x2Tq[q]; x2t = x2tq[q]
                for ti in range(nt):
                    gbuild(q, ti)
                def lane_gen(pidx=pidx):
                    if pidx >= 1:
                        yield from epilogue_gen(pidx - 1)
                    if pidx + 1 < len(passes):
                        yield from route_pass(pidx + 1)
                nxt_route = lane_gen()
                load_u(0); load_u(1); load_v(0)

                def emit_hd(ka):
                    g, b = ka // NBK, ka % NBK
                    sl = g % NUB
                    ph = hbanks[ka % len(hbanks)]; hb = hg[ka % NHB]; hgb = hgg[ka % NHB]
                    for k in range(8):
                        op("pe", lambda ph=ph, k=k, b=b, sl=sl: PE.matmul(ph.t[:, 0:NT], ubuf[sl].t[:, b, k * 128:(k + 1) * 128], x2T.t[:, k, 0:NT], start=(k == 0), stop=(k == 7)),
                           reads=[ubuf[sl].k, x2T.k], writes=[ph.k])
                    op("act", lambda ph=ph, hb=hb: A.activation(hb.t[:, 0:NT], ph.t[:, 0:NT], AF.Gelu), reads=[ph.k], writes=[hb.k])
                    op("pool", lambda hb=hb, hgb=hgb, ka=ka: G.tensor_mul(hgb.t[:, 0:NT], hb.t[:, 0:NT], GTall.t[:, 0:NT, ka]), reads=[hb.k, GTall.k], writes=[hgb.k])

                def emit_v(ka):
                    g, b = ka // NBK, ka % NBK
                    sl = g % NUB
                    hgb = hgg[ka % NHB]
                    for ti in range(nt):
                        for c in range(2):
                            pa_ = psA[ti * 2 + c]
                            op("pe", lambda pa_=pa_, hgb=hgb, ti=ti, c=c, b=b, sl=sl: PE.matmul(pa_.t[:], hgb.t[:, ti * 128:(ti + 1) * 128], vbuf[sl].t[:, b, c * 512:(c + 1) * 512],
                                                                                         start=(ka == 0), stop=(ka == 127)),
                               reads=[hgb.k, vbuf[sl].k], writes=[pa_.k])

                assert NBK == 2
                for step in range(128 + LA):
                    if step % NBK == 0:
                        gq = step // NBK
                        if gq + 2 < NG:
                            load_u(gq + 2)
                        if gq + 1 < NG:
                            load_v(gq + 1)
                    if step < 128:
                        emit_hd(step)
                    kv = step - LA
                    if kv >= 0:
                        emit_v(kv)
                    for _ in range(2 if step % 2 else 1):
                        next(nxt_route, None)
                for _ in nxt_route:
                    pass
                for ti in range(nt):
                    for c in range(2):
                        pa_ = psA[ti * 2 + c]
                        op("dve", lambda pa_=pa_, ti=ti, c=c: V.scalar_tensor_tensor(x2t[ti].t[:, c * 512:(c + 1) * 512], x2t[ti].t[:, c * 512:(c + 1) * 512], ALPHA, pa_.t[:], ALU.mult, ALU.add),
                           reads=[x2t[ti].k, pa_.k], writes=[x2t[ti].k])
            for _ in epilogue_gen(len(passes) - 1):
                pass
            P.barrier()
    return nc, P


_IN_NAMES = None


def shard_inputs(inp, NP, NS, ncores):
    maps = []
    f = np.ascontiguousarray
    for c in range(ncores):
        ps_ = slice(c * NP, (c + 1) * NP)
        ss_ = slice(c * NS, (c + 1) * NS)
        m = {
            "x_p": f(inp["x_prompt"][ps_]),
            "x_s": f(inp["x_sample"][ss_].reshape(-1, D)),
            "mem_p": f(inp["mem_prompt"][ps_]),
            "c_sbk": f(inp["cache_sb_k"][0, ss_].reshape(NS, -1, 512)),
            "c_sbv": f(inp["cache_sb_v"][0, ss_].reshape(NS, -1, 512)),
            "c_fxk": f(inp["cache_fox_k"][0, ss_].reshape(NS, -1, 512)),
            "c_fxv": f(inp["cache_fox_v"][0, ss_].reshape(NS, -1, 512)),
            "c_lf": f(inp["cache_fox_logf"][0, ss_]),
            "c_mk": f(inp["cache_mem_k"][0, ss_].reshape(NS, NMEM, D)),
            "c_mv": f(inp["cache_mem_v"][0, ss_].reshape(NS, NMEM, D)),
            "w_in": f(inp["w_in"][0]), "b_f": f(inp["b_f"][0].reshape(1, 8)), "w_gn": f(inp["w_gn"][0].reshape(D, 1)),
            "w_out": f(inp["w_out"][0]), "ln1_g": f(inp["ln1_g"][0].reshape(1, D)), "ln1_b": f(inp["ln1_b"][0].reshape(1, D)),
            "w_mq": f(inp["w_mq"][0]), "w_mk": f(inp["w_mk"][0]), "w_mv": f(inp["w_mv"][0]), "w_mo": f(inp["w_mo"][0]),
            "ln2_g": f(inp["ln2_g"][0].reshape(1, D)), "ln2_b": f(inp["ln2_b"][0].reshape(1, D)),
            "w_pq": f(inp["w_pq"][0]), "keys_a": f(inp["peer_keys_a"][0]), "keys_b": f(inp["peer_keys_b"][0]),
            "pu": f(inp["peer_u"][0]), "pv": f(inp["peer_v"][0]),
            "ln3_g": f(inp["ln3_g"][0].reshape(1, D)), "ln3_b": f(inp["ln3_b"][0].reshape(1, D)),
        }
        maps.append({k: np.asarray(v, dtype=np.float32) for k, v in m.items()})
    return maps


def gather_outputs(results, NP, T, NS, TS):
    def cat(name, shape_per_core):
        return np.concatenate([np.asarray(r[name]).reshape(shape_per_core) for r in results], axis=0)
    B = NP * len(results)
    Bs = NS * len(results)
    y_p = cat("y_p", (NP, T, D)); y_s = cat("y_s", (NS, TS, D))
    outs = [y_p, y_s]
    for nm in ("o_sbk_p", "o_sbv_p", "o_fxk_p", "o_fxv_p"):
        outs.append(cat(nm, (NP, T, 8, HD))[None])
    outs.append(cat("o_lf_p", (NP, T, 8))[None])
    for nm in ("o_mk_p", "o_mv_p"):
        outs.append(cat(nm, (NP, NMEM, 4, 256))[None])
    for nm in ("o_sbk_s", "o_sbv_s", "o_fxk_s", "o_fxv_s"):
        outs.append(cat(nm, (NS, TS, 8, HD))[None])
    outs.append(cat("o_lf_s", (NS, TS, 8))[None])
    return tuple(np.ascontiguousarray(o, dtype=np.float32) for o in outs)


def kernel(**inputs):
    ncores = 8
    B, T, _ = inputs["x_prompt"].shape
    Bs, TS, _ = inputs["x_sample"].shape
    PAST = inputs["cache_sb_k"].shape[2]
    NP, NS = B // ncores, Bs // ncores
    nc, _ = build(NP, T, NS, TS, PAST)
    maps = shard_inputs(inputs, NP, NS, ncores)
    res = run_bass_kernel_spmd(nc, maps, core_ids=list(range(ncores)))
    return gather_outputs(res.results, NP, T, NS, TS)
```

```python
import os
import numpy as np
from contextlib import ExitStack
import concourse.bass as bass
import concourse.mybir as mybir
from concourse.bass_utils import run_bass_kernel_spmd

F32 = mybir.dt.float32
BF16 = mybir.dt.bfloat16
U32 = mybir.dt.uint32
I32 = mybir.dt.int32
AF = mybir.ActivationFunctionType
ALU = mybir.AluOpType
AX = mybir.AxisListType

D = 1024
HD = 64
NMEM = 256
ALPHA = 2.0 ** 0.25
LN_EPS = 1e-5
GN_EPS = 1e-6
NEXP = 16384


class DSem:
    __slots__ = ("sem", "total")

    def __init__(self, sem):
        self.sem = sem
        self.total = 0


class Tok:
    __slots__ = ("w", "r", "dw", "dr", "own", "own_sw", "name", "excl")

    def __init__(self, name=""):
        self.w = None
        self.r = {}
        self.dw = []
        self.dr = []
        self.own = None
        self.own_sw = None
        self.name = name
        self.excl = False


class Buf:
    __slots__ = ("t", "k")

    def __init__(self, t, k):
        self.t = t
        self.k = k


class Prog:
    def __init__(self, nc):
        self.nc = nc
        self.es = ExitStack()
        self.eng = {"pe": nc.tensor, "act": nc.scalar, "dve": nc.vector,
                    "pool": nc.gpsimd, "sp": nc.sync}
        self.sem = {}
        self.cnt = {}
        self.waited = {}
        for k in self.eng:
            self.sem[k] = self.es.enter_context(nc.semaphore("c_" + k))
            self.cnt[k] = 0
            self.waited[k] = {}
        self.dsems = []
        self.ninst = 0
        self.uid = 0
        self.limit = None
        self.marks = []

    def mark(self, name):
        self.marks.append((name, self.ninst))

    def sb(self, es, name, shape, dt, dma=False):
        self.uid += 1
        t = es.enter_context(self.nc.sbuf_tensor("%s_%d" % (name, self.uid), list(shape), dt))
        return Buf(t, self.dtok(name) if dma else Tok(name))

    def ps(self, es, name, shape, dt):
        self.uid += 1
        t = es.enter_context(self.nc.psum_tensor("%s_%d" % (name, self.uid), list(shape), dt))
        b = Buf(t, Tok(name))
        b.k.excl = True
        return b

    def dtok(self, name=""):
        t = Tok(name)
        self.uid += 1
        t.own = DSem(self.es.enter_context(self.nc.semaphore("d%d" % self.uid)))
        self.dsems.append(t.own)
        return t

    def _need(self, e, key, sem, val):
        if e == "pe" and key == "pe":
            return
        w = self.waited[e]
        if w.get(key, 0) >= val:
            return
        w[key] = val
        self.eng[e].wait_ge(sem, val)

    def _deps(self, e, reads, writes):
        for t in reads:
            if t.w is not None:
                self._need(e, t.w[0], self.sem[t.w[0]], t.w[1])
            for d in t.dw:
                self._need(e, id(d), d.sem, d.total)
        for t in writes:
            if t.w is not None:
                self._need(e, t.w[0], self.sem[t.w[0]], t.w[1])
            for re_, rc in t.r.items():
                self._need(e, re_, self.sem[re_], rc)
            for d in t.dw + t.dr:
                self._need(e, id(d), d.sem, d.total)

    def op(self, e, fn, reads=(), writes=()):
        if self.limit is not None and self.ninst >= self.limit:
            return None
        ex = [t for t in reads if t.excl]
        if ex:
            reads = [t for t in reads if not t.excl]
            writes = list(writes) + [t for t in ex if t not in writes]
        self._deps(e, reads, writes)
        ins = fn()
        self.cnt[e] += 1
        c = self.cnt[e]
        ins.then_inc(self.sem[e], 1)
        for t in writes:
            t.w = (e, c)
            t.r = {}
            t.dw = []
            t.dr = []
        for t in reads:
            if t.r.get(e, 0) < c:
                t.r[e] = c
        self.ninst += 1
        return ins

    def dma(self, q, out, in_, semtok, reads=(), writes=(), **kw):
        if self.limit is not None and self.ninst >= self.limit:
            return None
        self._deps(q, reads, writes)
        ins = self.eng[q].dma_start(out=out, in_=in_, **kw)
        if q == "pool":
            if semtok.own_sw is None:
                self.uid += 1
                semtok.own_sw = DSem(self.es.enter_context(self.nc.semaphore("w%d" % self.uid)))
                self.dsems.append(semtok.own_sw)
            d = semtok.own_sw
        else:
            d = semtok.own
        ins.then_inc(d.sem, 16)
        d.total += 16
        for t in writes:
            t.w = None
            t.r = {}
            t.dw = [d]
            t.dr = []
        if not writes and semtok.dw and d not in semtok.dw:
            semtok.dw.append(d)
        for t in reads:
            if d not in t.dr:
                t.dr.append(d)
        self.ninst += 1
        return ins

    def barrier(self):
        for e in self.eng:
            for k in ("pe", "act", "dve", "pool"):
                if self.cnt[k]:
                    self._need(e, k, self.sem[k], self.cnt[k])
            for d in self.dsems:
                if d.total:
                    self._need(e, id(d), d.sem, d.total)


def build(NP, T, NS, TS, PAST, stages="UAaMB", dbg=False):
    assert T % 128 == 0 and PAST % 128 == 0 and NS * TS == 128 and TS == 64
    nc = bass.Bass("TRN2", target_bir_lowering=False)
    P = Prog(nc)
    if os.environ.get("KLIMIT"):
        P.limit = int(os.environ["KLIMIT"])
    op, dma = P.op, P.dma
    V, A, G, PE = nc.vector, nc.scalar, nc.gpsimd, nc.tensor
    KT = T // 128
    PT_ = PAST // 128
    NTOK = NP * T + NS * TS
    NTILE = NTOK // 128
    scale = HD ** -0.5

    def din(name, shape):
        return nc.dram_tensor(name, list(shape), F32, kind="ExternalInput").ap()

    def dout(name, shape):
        return nc.dram_tensor(name, list(shape), F32, kind="ExternalOutput").ap()

    def dscr(name, shape, dt):
        return nc.dram_tensor(name, list(shape), dt, kind="Internal").ap()

    x_p = din("x_p", [NP, T, D]); x_s = din("x_s", [NS * TS, D]); mem_p = din("mem_p", [NP, NMEM, D])
    c_sbk = din("c_sbk", [NS, PAST, 512]); c_sbv = din("c_sbv", [NS, PAST, 512])
    c_fxk = din("c_fxk", [NS, PAST, 512]); c_fxv = din("c_fxv", [NS, PAST, 512])
    c_lf = din("c_lf", [NS, PAST, 8]); c_mk = din("c_mk", [NS, NMEM, D]); c_mv = din("c_mv", [NS, NMEM, D])
    w_in = din("w_in", [D, 3080]); b_f = din("b_f", [1, 8]); w_gn = din("w_gn", [D, 1]); w_out = din("w_out", [D, D])
    ln1_g = din("ln1_g", [1, D]); ln1_b = din("ln1_b", [1, D]); w_mq = din("w_mq", [D, D]); w_mk = din("w_mk", [D, D])
    w_mv = din("w_mv", [D, D]); w_mo = din("w_mo", [D, D]); ln2_g = din("ln2_g", [1, D]); ln2_b = din("ln2_b", [1, D])
    w_pq = din("w_pq", [D, 2048]); keys_a = din("keys_a", [8, 128, 128]); keys_b = din("keys_b", [8, 128, 128])
    pu = din("pu", [NEXP, D]); pv = din("pv", [NEXP, D]); ln3_g = din("ln3_g", [1, D]); ln3_b = din("ln3_b", [1, D])

    y_p = dout("y_p", [NP * T, D]); y_s = dout("y_s", [NS * TS, D])
    o_sbk_p = dout("o_sbk_p", [NP * T, 512]); o_sbv_p = dout("o_sbv_p", [NP * T, 512])
    o_fxk_p = dout("o_fxk_p", [NP * T, 512]); o_fxv_p = dout("o_fxv_p", [NP * T, 512])
    o_lf_p = dout("o_lf_p", [NP * T, 8]); o_mk_p = dout("o_mk_p", [NP * NMEM, D]); o_mv_p = dout("o_mv_p", [NP * NMEM, D])
    o_sbk_s = dout("o_sbk_s", [NS * TS, 512]); o_sbv_s = dout("o_sbv_s", [NS * TS, 512])
    o_fxk_s = dout("o_fxk_s", [NS * TS, 512]); o_fxv_s = dout("o_fxv_s", [NS * TS, 512])
    o_lf_s = dout("o_lf_s", [NS * TS, 8])

    uT_s = dscr("uT_s", [128, 128, 1024], BF16)
    v_s = dscr("v_s", [NEXP, D], BF16)
    O_s = (dout if dbg else (lambda n, sh: dscr(n, sh, F32)))("O_s", [NTOK, D])
    x2_s = (dout if dbg else (lambda n, sh: dscr(n, sh, F32)))("x2_s", [NTOK, D])
    t_Os = P.dtok("O_s"); t_x2s = P.dtok("x2_s"); t_uTs = P.dtok("uT_s"); t_vs = P.dtok("v_s")
    t_out = P.dtok("outs")

    ces = P.es
    idf = P.sb(ces, "idf", [128, 128], F32); idb = P.sb(ces, "idb", [128, 128], BF16)
    m_lt_f = P.sb(ces, "m_lt_f", [128, 128], F32); m_lt_b = P.sb(ces, "m_lt_b", [128, 128], BF16)
    m_ge_f = P.sb(ces, "m_ge_f", [128, 128], F32); m_ge_b = P.sb(ces, "m_ge_b", [128, 128], BF16)
    ones_f = P.sb(ces, "ones_f", [128, 128], F32)
    iota_i = P.sb(ces, "iota_i", [128, 128], I32); iota_f = P.sb(ces, "iota_f", [128, 128], F32)
    op("pool", lambda: G.memset(idf.t[:], 0.0), writes=[idf.k])
    op("pool", lambda: G.affine_select(idf.t[:], idf.t[:], [[-1, 128]], ALU.not_equal, 1.0, base=0, channel_multiplier=1),
       reads=[idf.k], writes=[idf.k])
    op("dve", lambda: V.tensor_copy(idb.t[:], idf.t[:]), reads=[idf.k], writes=[idb.k])
    op("pool", lambda: G.memset(ones_f.t[:], 1.0), writes=[ones_f.k])
    op("pool", lambda: G.affine_select(m_lt_f.t[:], ones_f.t[:], [[-1, 128]], ALU.is_gt, 0.0, base=0, channel_multiplier=1),
       reads=[ones_f.k], writes=[m_lt_f.k])
    op("dve", lambda: V.tensor_copy(m_lt_b.t[:], m_lt_f.t[:]), reads=[m_lt_f.k], writes=[m_lt_b.k])
    op("pool", lambda: G.affine_select(m_ge_f.t[:], ones_f.t[:], [[1, 128]], ALU.is_ge, 0.0, base=0, channel_multiplier=-1),
       reads=[ones_f.k], writes=[m_ge_f.k])
    op("dve", lambda: V.tensor_copy(m_ge_b.t[:], m_ge_f.t[:]), reads=[m_ge_f.k], writes=[m_ge_b.k])
    op("pool", lambda: G.iota(iota_i.t[:], [[1, 128]], base=0, channel_multiplier=0), writes=[iota_i.k])
    op("dve", lambda: V.tensor_copy(iota_f.t[:], iota_i.t[:]), reads=[iota_i.k], writes=[iota_f.k])

    flip = [0]

    def evac(out_ap, in_ap, reads, writes):
        flip[0] ^= 1
        if flip[0]:
            op("act", lambda: A.copy(out_ap, in_ap), reads=reads, writes=writes)
        else:
            op("dve", lambda: V.tensor_copy(out_ap, in_ap), reads=reads, writes=writes)

    def load_w_bf16(dst, src, ncols, extra_reads=()):
        c0 = 0
        first = True
        while c0 < ncols:
            c1 = min(ncols, c0 + 1024)
            for k0 in range(0, 8, 2):
                dma("pool", dst.t[:, k0:k0 + 2, c0:c1], src[k0 * 128:(k0 + 2) * 128, c0:c1].rearrange("(k p) c -> p k c", p=128), dst.k,
                    writes=[dst.k] if first else [])
                first = False
            c0 = c1

    def layer_norm(es_tmp, r, g_b, b_b, outb, nrows=128):
        st = lnw["st"]; mv = lnw["mv"]; rs = lnw["rs"]
        for c in range(2):
            op("dve", lambda c=c: V.bn_stats(st.t[:, c, :], r.t[:, c * 512:(c + 1) * 512]), reads=[r.k], writes=[st.k])
        op("dve", lambda: V.bn_aggr(mv.t[:], st.t[:]), reads=[st.k], writes=[mv.k])
        op("dve", lambda: V.tensor_scalar(rs.t[:], mv.t[:, 1:2], LN_EPS, None, ALU.add), reads=[mv.k], writes=[rs.k])
        op("act", lambda: A.sqrt(rs.t[:], rs.t[:]), reads=[rs.k], writes=[rs.k])
        op("dve", lambda: V.reciprocal(rs.t[:], rs.t[:]), reads=[rs.k], writes=[rs.k])
        op("dve", lambda: V.tensor_scalar(r.t[:], r.t[:], mv.t[:, 0:1], rs.t[:], ALU.subtract, ALU.mult),
           reads=[r.k, mv.k, rs.k], writes=[r.k])
        op("dve", lambda: V.tensor_mul(r.t[:], r.t[:], g_b.t[:]), reads=[r.k, g_b.k], writes=[r.k])
        op("dve", lambda: V.tensor_add(outb.t[:], r.t[:], b_b.t[:]), reads=[r.k, b_b.k], writes=[outb.k])

    lnw = {}
    lnw["st"] = P.sb(ces, "ln_st", [128, 2, 6], F32)
    lnw["mv"] = P.sb(ces, "ln_mv", [128, 2], F32)
    lnw["rs"] = P.sb(ces, "ln_rs", [128, 1], F32)

    def u_gen(es, banks, NB):
        ust = [P.sb(es, "ust", [128, NB, D], F32, dma=True) for _ in range(2)]
        uTb = [P.sb(es, "uTb", [128, NB, 8, 128], BF16, dma=True) for _ in range(2)]
        vb = [P.sb(es, "vb", [128, NB, D], BF16, dma=True) for _ in range(2)]
        pi = 0
        ngrp = 128 // NB

        def loads(g):
            s = g % 2
            rows = slice(g * NB * 128, (g + 1) * NB * 128)
            dma("sp", ust[s].t[:], pu[rows, :].rearrange("(b e) d -> e b d", e=128), ust[s].k, writes=[ust[s].k])
            dma("pool", vb[s].t[:], pv[rows, :].rearrange("(b e) d -> e b d", e=128), vb[s].k, writes=[vb[s].k])

        loads(0)
        for g in range(ngrp):
            s = g % 2
            rows = slice(g * NB * 128, (g + 1) * NB * 128)
            if g + 1 < ngrp:
                loads(g + 1)
            yield
            for b in range(NB):
                for hf in range(2):
                    pb = banks[pi % len(banks)]; pi += 1
                    for i in range(4):
                        k = hf * 4 + i
                        op("pe", lambda pb=pb, i=i, k=k, b=b: PE.transpose(pb.t[:, i * 128:(i + 1) * 128],
                                                                          ust[s].t[:, b, k * 128:(k + 1) * 128], idf.t[:]),
                           reads=[ust[s].k, idf.k], writes=[pb.k])
                    evac(uTb[s].t[:, b, hf * 4:(hf + 1) * 4, :], pb.t[:].rearrange("p (k e) -> p k e", e=128),
                         reads=[pb.k], writes=[uTb[s].k])
                    yield
            dma("sp", v_s[rows, :].rearrange("(b e) d -> e b d", e=128), vb[s].t[:], vb[s].k, reads=[vb[s].k], writes=[t_vs])
            yield
            dma("sp", uT_s[g * NB:(g + 1) * NB].rearrange("b p f -> p b f"),
                uTb[s].t[:].rearrange("p b k e -> p b (k e)"), uTb[s].k, reads=[uTb[s].k], writes=[t_uTs])
            yield

    u_overlapped = bool(NS) and "a" in stages and "U" in stages
    if "U" in stages and not u_overlapped:
        with ExitStack() as es:
            psU = [P.ps(es, "psU%d" % i, [128, 512], F32) for i in range(4)]
            for _ in u_gen(es, psU, 4):
                pass
            P.barrier()

    def stageA(prompt):
      with ExitStack() as es:
          psM = [P.ps(es, "psM%d" % i, [128, 512], F32) for i in range(2)]
          psT = [P.ps(es, "psT%d" % i, [128, 512], F32) for i in range(2)]
          psO = [P.ps(es, "psO%d" % i, [128, 512], F32) for i in range(2)]
          psB = [P.ps(es, "psB%d" % i, [128, 1024], BF16) for i in range(2)]
          rr = {"M": 0, "T": 0, "O": 0, "B": 0}

          def nxt(lst, key):
              b = lst[rr[key] % len(lst)]
              rr[key] += 1
              return b

          win = P.sb(es, "win", [128, 8, 3080], BF16, dma=True)
          load_w_bf16(win, w_in, 3080)
          bfb = P.sb(es, "bfb", [128, 8], F32, dma=True)
          dma("sp", bfb.t[:], b_f.partition_broadcast(128), bfb.k, writes=[bfb.k])
          xt = P.sb(es, "xt", [128, D], F32, dma=True)
          xT = P.sb(es, "xT", [128, 8, 128], BF16)
          pj = P.sb(es, "pj", [128, 3080], F32, dma=True)
          lft = P.sb(es, "lft", [128, 8], F32, dma=True)
          lfw = P.sb(es, "lfw", [128, 8], F32)
          accs = P.sb(es, "accs", [128, 8], F32)
          rbt = P.sb(es, "rbt", [128, 8], F32)
          qTs = P.sb(es, "qTs", [128, 4, 128], BF16); qTf = P.sb(es, "qTf", [128, 4, 128], BF16)
          Ot = P.sb(es, "Ot", [128, D], F32, dma=True)
          et = P.sb(es, "et", [128, 512], F32)
          WT = [P.sb(es, "WT", [128, 4, 128], BF16) for _ in range(2)]
          PTt = [P.sb(es, "PTt", [128, 128], BF16) for _ in range(2)]
          ntot = P.sb(es, "ntot", [128, 1], F32)
          rden = P.sb(es, "rden", [128, 1], F32)
          KMAX = T if prompt else PAST + TS
          SPb = P.sb(es, "SPb", [128, KMAX], F32)
          ZSb = P.sb(es, "ZSb", [128, KMAX], F32)
          Db = P.sb(es, "Db", [128, KMAX + 1], F32)
          Wb = P.sb(es, "Wb", [128, KMAX], BF16)
          op("dve", lambda: V.memset(Db.t[:, 0:1], 0.0), writes=[Db.k])
          if prompt:
              kTs = P.sb(es, "kTs", [128, 4, T], BF16); kTf = P.sb(es, "kTf", [128, 4, T], BF16)
              vS = P.sb(es, "vS", [128, KT, 512], BF16); vF = P.sb(es, "vF", [128, KT, 8, 65], BF16)
              call = P.sb(es, "call", [128, KT, 8], F32)
              biasb = P.sb(es, "biasb", [128, KT, 8], F32)

          def project_tile(x_rows, outs, row0):
              dma("sp", xt.t[:], x_rows, xt.k, writes=[xt.k])
              for hf in range(2):
                  pb = nxt(psT, "T")
                  for i in range(4):
                      k = hf * 4 + i
                      op("pe", lambda pb=pb, i=i, k=k: PE.transpose(pb.t[:, i * 128:(i + 1) * 128], xt.t[:, k * 128:(k + 1) * 128], idf.t[:]),
                         reads=[xt.k, idf.k], writes=[pb.k])
                  evac(xT.t[:, hf * 4:(hf + 1) * 4, :], pb.t[:].rearrange("p (k e) -> p k e", e=128), reads=[pb.k], writes=[xT.k])
              c0 = 0
              while c0 < 3080:
                  c1 = min(3080, c0 + 512)
                  pb = nxt(psM, "M")
                  for k in range(8):
                      op("pe", lambda pb=pb, k=k, c0=c0, c1=c1: PE.matmul(pb.t[:, 0:c1 - c0], xT.t[:, k, :], win.t[:, k, c0:c1],
                                                                         start=(k == 0), stop=(k == 7)),
                         reads=[xT.k, win.k], writes=[pb.k])
                  evac(pj.t[:, c0:c1], pb.t[:, 0:c1 - c0], reads=[pb.k], writes=[pj.k])
                  c0 = c1
              o_k1, o_v1, o_k2, o_v2, o_lf = outs
              rs_ = slice(row0, row0 + 128)
              dma("pool", o_k1[rs_, :], pj.t[:, 512:1024], t_out, reads=[pj.k])
              dma("pool", o_v1[rs_, :], pj.t[:, 1024:1536], t_out, reads=[pj.k])
              dma("pool", o_k2[rs_, :], pj.t[:, 2048:2560], t_out, reads=[pj.k])
              dma("pool", o_v2[rs_, :], pj.t[:, 2560:3072], t_out, reads=[pj.k])
              op("dve", lambda: V.tensor_add(lfw.t[:], pj.t[:, 3072:3080], bfb.t[:]), reads=[pj.k, bfb.k], writes=[lfw.k])
              op("act", lambda: A.activation(lfw.t[:], lfw.t[:], AF.Exp, scale=-1.0), reads=[lfw.k], writes=[lfw.k])
              op("act", lambda: A.activation(lfw.t[:], lfw.t[:], AF.Ln, bias=1.0), reads=[lfw.k], writes=[lfw.k])
              op("dve", lambda: V.tensor_scalar(lft.t[:], lfw.t[:], -1.0, None, ALU.mult), reads=[lfw.k], writes=[lft.k])
              dma("pool", o_lf[rs_, :], lft.t[:], t_out, reads=[lft.k])

          def transposes_qk(dst_q_s, dst_k_s, dst_q_f, dst_k_f):
              for (c0, dst) in ((0, dst_q_s), (512, dst_k_s), (1536, dst_q_f), (2048, dst_k_f)):
                  pb = nxt(psT, "T")
                  for i in range(4):
                      op("pe", lambda pb=pb, i=i, c0=c0: PE.transpose(pb.t[:, i * 128:(i + 1) * 128], pj.t[:, c0 + i * 128:c0 + (i + 1) * 128], idf.t[:]),
                         reads=[pj.k, idf.k], writes=[pb.k])
                  evac(dst[0], pb.t[:].rearrange("p (k e) -> p k e", e=128), reads=[pb.k], writes=[dst[1]])

          def attend_sb(S, nq, klen, qT_ap, q_tok, kT_ap, k_tok, v_of, v_tok, O_ap):
              et, SPb, ZSb, Db, Wb, ntot, WT, po = S["et"], S["SPb"], S["ZSb"], S["Db"], S["Wb"], S["ntot"], S["WT"], S["po"]
              c0 = 0
              while c0 < klen:
                  w = min(512, klen - c0)
                  pb = nxt(psM, "M")
                  op("pe", lambda pb=pb, c0=c0, w=w: PE.matmul(pb.t[0:nq, 0:w], qT_ap, kT_ap[:, c0:c0 + w], start=True, stop=True),
                     reads=[q_tok, k_tok], writes=[pb.k])
                  op("act", lambda pb=pb, w=w: A.activation(et.t[0:nq, 0:w], pb.t[0:nq, 0:w], AF.Exp, scale=scale), reads=[pb.k], writes=[et.k])
                  op("dve", lambda pb=pb, c0=c0, w=w: V.tensor_scalar(ZSb.t[0:nq, c0:c0 + w], pb.t[0:nq, 0:w], scale, None, ALU.mult),
                     reads=[pb.k], writes=[ZSb.k])
                  op("act", lambda c0=c0, w=w: A.activation(SPb.t[0:nq, c0:c0 + w], et.t[0:nq, 0:w], AF.Ln, bias=1.0), reads=[et.k], writes=[SPb.k])
                  c0 += w
                  yield
              d0 = klen - nq
              op("dve", lambda: V.tensor_mul(SPb.t[0:nq, d0:klen], SPb.t[0:nq, d0:klen], m_lt_f.t[0:nq, 0:nq]),
                 reads=[SPb.k, m_lt_f.k], writes=[SPb.k])
              op("dve", lambda: V.tensor_tensor_scan(Db.t[0:nq, 1:klen + 1], SPb.t[0:nq, 0:klen], SPb.t[0:nq, 0:klen], 0.0, ALU.add, ALU.max),
                 reads=[SPb.k], writes=[Db.k])
              yield
              op("dve", lambda: V.tensor_scalar(ntot.t[0:nq, :], Db.t[0:nq, klen:klen + 1], -1.0, None, ALU.mult), reads=[Db.k], writes=[ntot.k])
              op("dve", lambda: V.tensor_add(ZSb.t[0:nq, 0:klen], ZSb.t[0:nq, 0:klen], Db.t[0:nq, 0:klen]), reads=[ZSb.k, Db.k], writes=[ZSb.k])
              yield
              op("act", lambda: A.activation(Wb.t[0:nq, 0:klen], ZSb.t[0:nq, 0:klen], AF.Exp, bias=ntot.t[0:nq, :]),
                 reads=[ZSb.k, ntot.k], writes=[Wb.k])
              yield
              op("dve", lambda: V.tensor_mul(Wb.t[0:nq, d0:klen], Wb.t[0:nq, d0:klen], m_lt_b.t[0:nq, 0:nq]),
                 reads=[Wb.k, m_lt_b.k], writes=[Wb.k])
              nkt = (klen + 127) // 128
              a = 0
              gi = 0
              while a < nkt:
                  na = min(4, nkt - a)
                  pb = nxt(psB, "B"); wt = WT[gi % 2]; gi += 1
                  for i in range(na):
                      ks = min(128, klen - (a + i) * 128)
                      op("pe", lambda pb=pb, i=i, a=a, ks=ks: PE.transpose(pb.t[0:ks, i * 128:i * 128 + nq], Wb.t[0:nq, (a + i) * 128:(a + i) * 128 + ks], idb.t[0:nq, 0:nq]),
                         reads=[Wb.k, idb.k], writes=[pb.k])
                  ksl = min(128, klen - (a + na - 1) * 128)
                  if ksl == 128:
                      evac(wt.t[:, 0:na, 0:nq], pb.t[:, 0:na * 128].rearrange("p (k e) -> p k e", e=128)[:, :, 0:nq], reads=[pb.k], writes=[wt.k])
                  else:
                      if na > 1:
                          evac(wt.t[:, 0:na - 1, 0:nq], pb.t[:, 0:(na - 1) * 128].rearrange("p (k e) -> p k e", e=128)[:, :, 0:nq], reads=[pb.k], writes=[wt.k])
                      evac(wt.t[0:ksl, na - 1, 0:nq], pb.t[0:ksl, (na - 1) * 128:(na - 1) * 128 + nq], reads=[pb.k, wt.k], writes=[wt.k])
                  yield
                  for i in range(na):
                      ks = min(128, klen - (a + i) * 128)
                      op("pe", lambda wt=wt, i=i, a=a, ks=ks: PE.matmul(po.t[0:nq, 0:64], wt.t[0:ks, i, 0:nq], v_of(a + i, ks),
                                                                     start=(a + i == 0), stop=(a + i == nkt - 1)),
                         reads=[wt.k, v_tok], writes=[po.k])
                  a += na
              yield
              op("act", lambda: A.copy(O_ap, po.t[0:nq, 0:64]), reads=[po.k], writes=[Ot.k])

          def attend_fx(F, nq, klen, qT_ap, q_tok, kT_ap, k_tok, vaug_of, v_tok, bias_of, b_tok, O_ap):
              PTt, rden, po = F["PTt"], F["rden"], F["po"]
              nkt = (klen + 127) // 128
              for a in range(nkt):
                  ks = min(128, klen - a * 128)
                  pb = nxt(psM, "M"); pt = PTt[a % len(PTt)]
                  op("pe", lambda pb=pb, a=a, ks=ks: PE.matmul(pb.t[0:ks, 0:nq], kT_ap[:, a * 128:a * 128 + ks], qT_ap, start=True, stop=True),
                     reads=[q_tok, k_tok], writes=[pb.k])
                  op("act", lambda pb=pb, pt=pt, a=a, ks=ks: A.activation(pt.t[0:ks, 0:nq], pb.t[0:ks, 0:nq], AF.Exp, bias=bias_of(a, ks), scale=scale),
                     reads=[pb.k, b_tok], writes=[pt.k])
                  if a == nkt - 1:
                      op("dve", lambda pt=pt, ks=ks: V.tensor_mul(pt.t[0:ks, 0:nq], pt.t[0:ks, 0:nq], m_ge_b.t[0:ks, 0:nq]),
                         reads=[pt.k, m_ge_b.k], writes=[pt.k])
                  yield
                  op("pe", lambda pt=pt, a=a, ks=ks: PE.matmul(po.t[0:nq, 0:65], pt.t[0:ks, 0:nq], vaug_of(a, ks), start=(a == 0), stop=(a == nkt - 1)),
                     reads=[pt.k, v_tok], writes=[po.k])
              yield
              op("dve", lambda: V.reciprocal(rden.t[0:nq, :], po.t[0:nq, 64:65]), reads=[po.k], writes=[rden.k])
              op("dve", lambda: V.tensor_scalar(O_ap, po.t[0:nq, 0:64], rden.t[0:nq, :], None, ALU.mult), reads=[po.k, rden.k], writes=[Ot.k])

          def chain(gens):
              for g_ in gens:
                  yield from g_

          bg = [None]

          def run_lanes(lanes):
              lanes = [l for l in lanes]
              while lanes:
                  for l in list(lanes):
                      try:
                          next(l)
                      except StopIteration:
                          lanes.remove(l)
                  if bg[0] is not None:
                      next(bg[0], None)

          nset = 2 if prompt else 1
          sbsets = []
          for i_ in range(nset):
              if i_ == 0:
                  S_ = dict(et=et, SPb=SPb, ZSb=ZSb, Db=Db, Wb=Wb, ntot=ntot, WT=WT, po=psO[0])
              else:
                  S_ = dict(et=P.sb(es, "et1", [128, 512], F32), SPb=P.sb(es, "SPb1", [128, KMAX], F32), ZSb=P.sb(es, "ZSb1", [128, KMAX], F32),
                            Db=P.sb(es, "Db1", [128, KMAX + 1], F32), Wb=P.sb(es, "Wb1", [128, KMAX], BF16), ntot=P.sb(es, "ntot1", [128, 1], F32),
                            WT=[P.sb(es, "WT1", [128, 4, 128], BF16) for _ in range(2)], po=psO[1])
                  op("dve", lambda S_=S_: V.memset(S_["Db"].t[:, 0:1], 0.0), writes=[S_["Db"].k])
              sbsets.append(S_)
          PT4 = PTt + [P.sb(es, "PTt2", [128, 128], BF16) for _ in range(2)]
          fxset = dict(PTt=PT4, rden=rden, po=psT[0] if prompt else psO[1])
          fxset2 = dict(PTt=[P.sb(es, "PTt3", [128, 128], BF16) for _ in range(4)], rden=P.sb(es, "rden2", [128, 1], F32), po=psT[1]) if prompt else None

          outs_p = (o_sbk_p, o_sbv_p, o_fxk_p, o_fxv_p, o_lf_p)
          for sq in range(NP if prompt else 0):
              op("dve", lambda: V.memset(accs.t[:], 0.0), writes=[accs.k])
              op("pool", lambda: G.memset(vF.t[:, :, :, 64:65], 1.0), writes=[vF.k])
              for j in range(KT):
                  row0 = sq * T + j * 128
                  project_tile(x_p[sq, j * 128:(j + 1) * 128, :], outs_p, row0)
                  transposes_qk((qTs.t[:], qTs.k), (kTs.t[:, :, j * 128:(j + 1) * 128], kTs.k),
                                (qTf.t[:], qTf.k), (kTf.t[:, :, j * 128:(j + 1) * 128], kTf.k))
                  op("act", lambda j=j: A.copy(vS.t[:, j, :], pj.t[:, 1024:1536]), reads=[pj.k], writes=[vS.k])
                  op("dve", lambda j=j: V.tensor_copy(vF.t[:, j, :, 0:64], pj.t[:, 2560:3072].rearrange("p (h d) -> p h d", d=64)),
                     reads=[pj.k], writes=[vF.k])
                  pb = nxt(psM, "M")
                  op("pe", lambda pb=pb: PE.matmul(pb.t[:, 0:8], m_ge_f.t[:], lft.t[:], start=True, stop=False), reads=[m_ge_f.k, lft.k], writes=[pb.k])
                  op("pe", lambda pb=pb: PE.matmul(pb.t[:, 0:8], ones_f.t[:], accs.t[:], start=False, stop=True), reads=[ones_f.k, accs.k], writes=[pb.k])
                  op("pe", lambda pb=pb: PE.matmul(pb.t[:, 8:16], ones_f.t[:], accs.t[:], start=True, stop=True), reads=[ones_f.k, accs.k], writes=[pb.k])
                  op("dve", lambda pb=pb, j=j: V.tensor_copy(call.t[:, j, :], pb.t[:, 0:8]), reads=[pb.k], writes=[call.k])
                  op("dve", lambda pb=pb: V.tensor_copy(rbt.t[:], pb.t[:, 8:16]), reads=[pb.k], writes=[rbt.k])
                  op("dve", lambda: V.tensor_add(accs.t[:], accs.t[:], lft.t[:]), reads=[accs.k, lft.k], writes=[accs.k])
                  op("dve", lambda j=j: V.tensor_tensor(biasb.t[:, 0:j + 1, :], rbt.t[:].unsqueeze(1).to_broadcast([128, j + 1, 8]),
                                                        call.t[:, 0:j + 1, :], ALU.subtract),
                     reads=[rbt.k, call.k], writes=[biasb.k])
                  klen = (j + 1) * 128
                  def sb_unit(S_, h):
                      p_, b0 = h // 2, 64 * (h % 2)
                      return attend_sb(S_, 128, klen, qTs.t[b0:b0 + 64, p_, :], qTs.k, kTs.t[b0:b0 + 64, p_, 0:klen], kTs.k,
                                       lambda a, ks, h=h: vS.t[0:ks, a, h * 64:(h + 1) * 64], vS.k, Ot.t[:, h * 64:(h + 1) * 64])

                  def fx_unit(h, fxs):
                      p_, b0 = h // 2, 64 * (h % 2)
                      return attend_fx(fxs, 128, klen, qTf.t[b0:b0 + 64, p_, :], qTf.k, kTf.t[b0:b0 + 64, p_, 0:klen], kTf.k,
                                       lambda a, ks, h=h: vF.t[0:ks, a, h, :], vF.k,
                                       lambda a, ks, h=h: biasb.t[0:ks, a, h:h + 1], biasb.k, Ot.t[:, 512 + h * 64:512 + (h + 1) * 64])

                  run_lanes([chain([sb_unit(sbsets[0], h) for h in (0, 2, 4, 6)]),
                             chain([sb_unit(sbsets[1], h) for h in (1, 3, 5, 7)]),
                             chain([fx_unit(h, fxset) for h in (0, 2, 4, 6)]),
                             chain([fx_unit(h, fxset2) for h in (1, 3, 5, 7)])])
                  dma("pool", O_s[row0:row0 + 128, :], Ot.t[:], Ot.k, reads=[Ot.k], writes=[t_Os])
          P.barrier()

          if not prompt:
              es2 = ExitStack()
              with es2:
                  KL = PAST + TS
                  kst = P.sb(es2, "kst", [128, PT_, 128], F32, dma=True)
                  kTp = P.sb(es2, "kTp", [128, KL], BF16)
                  vp = P.sb(es2, "vp", [128, PT_ + 1, 128], BF16, dma=True)
                  vpa = P.sb(es2, "vpa", [128, PT_ + 1, 2, 65], BF16)
                  vnew = P.sb(es2, "vnew", [128, 1024], BF16, dma=True)
                  lfp = P.sb(es2, "lfp", [128, PT_, 8], F32, dma=True)
                  lfn = P.sb(es2, "lfn", [128, 8], F32, dma=True)
                  cal2 = P.sb(es2, "cal2", [128, PT_ + 1, 8], F32)
                  tot2 = P.sb(es2, "tot2", [128, PT_ + 1, 8], F32)
                  bia2 = P.sb(es2, "bia2", [128, PT_ + 1, 8], F32)
                  kTn_s = P.sb(es2, "kTn_s", [128, 4, 128], BF16); kTn_f = P.sb(es2, "kTn_f", [128, 4, 128], BF16)
                  row0 = NP * T
                  if u_overlapped:
                      bg[0] = u_gen(es2, psT, 1)
                  op("pool", lambda: G.memset(vp.t[:, PT_, :], 0.0), writes=[vp.k])
                  project_tile(x_s[:, :], (o_sbk_s, o_sbv_s, o_fxk_s, o_fxv_s, o_lf_s), 0)
                  transposes_qk((qTs.t[:], qTs.k), (kTn_s.t[:], kTn_s.k), (qTf.t[:], qTf.k), (kTn_f.t[:], kTn_f.k))
                  op("act", lambda: A.copy(vnew.t[:, 0:512], pj.t[:, 1024:1536]), reads=[pj.k], writes=[vnew.k])
                  op("dve", lambda: V.tensor_copy(vnew.t[:, 512:1024], pj.t[:, 2560:3072]), reads=[pj.k], writes=[vnew.k])
                  op("pool", lambda: G.memset(vpa.t[:, :, :, 64:65], 1.0), writes=[vpa.k])
                  for si in range(NS):
                      q0 = si * TS
                      P.mark("sample%d_start" % si)
                      for a_ in range(0, PT_, 4):
                          a1_ = min(PT_, a_ + 4)
                          dma("sp", lfp.t[:, a_:a1_, :], c_lf[si][a_ * 128:a1_ * 128, :].rearrange("(a s) h -> s a h", s=128), lfp.k,
                              writes=[lfp.k] if a_ == 0 else [], allow_slow_non_contiguous=True)
                      dma("sp", lfn.t[0:TS, :], lft.t[q0:q0 + TS, :], lfn.k, reads=[lft.k], writes=[lfn.k])
                      pb = nxt(psM, "M")
                      op("pe", lambda pb=pb: PE.matmul(pb.t[:, 0:PT_ * 8], m_ge_f.t[:], lfp.t[:].rearrange("p a h -> p (a h)"), start=True, stop=True),
                         reads=[m_ge_f.k, lfp.k], writes=[pb.k])
                      pb2 = nxt(psM, "M")
                      op("pe", lambda pb2=pb2: PE.matmul(pb2.t[:, 0:PT_ * 8], ones_f.t[:], lfp.t[:].rearrange("p a h -> p (a h)"), start=True, stop=True),
                         reads=[ones_f.k, lfp.k], writes=[pb2.k])
                      op("dve", lambda: V.memset(tot2.t[:, 0, :], 0.0), writes=[tot2.k])
                      op("dve", lambda pb2=pb2: V.tensor_copy(cal2.t[:, 0:PT_, :], pb2.t[:, 0:PT_ * 8].rearrange("p (a h) -> p a h", h=8)),
                         reads=[pb2.k], writes=[cal2.k])
                      for h in range(8):
                          op("dve", lambda h=h: V.tensor_tensor_scan(tot2.t[:, 1:PT_ + 1, h], cal2.t[:, 0:PT_, h], cal2.t[:, 0:PT_, h], 0.0, ALU.add, ALU.min),
                             reads=[cal2.k], writes=[tot2.k])
                      op("dve", lambda pb=pb: V.tensor_tensor(cal2.t[:, 0:PT_, :], pb.t[:, 0:PT_ * 8].rearrange("p (a h) -> p a h", h=8),
                                                            tot2.t[:, 0:PT_, :], ALU.add), reads=[pb.k, tot2.k], writes=[cal2.k])
                      pb3 = nxt(psM, "M")
                      op("pe", lambda pb3=pb3: PE.matmul(pb3.t[0:TS, 0:8], m_ge_f.t[0:TS, 0:TS], lfn.t[0:TS, :], start=True, stop=True),
                         reads=[m_ge_f.k, lfn.k], writes=[pb3.k])
                      op("dve", lambda pb3=pb3: V.tensor_tensor(cal2.t[0:TS, PT_, :], pb3.t[0:TS, 0:8], tot2.t[0:TS, PT_, :], ALU.add),
                         reads=[pb3.k, tot2.k], writes=[cal2.k])
                      op("dve", lambda: V.tensor_tensor(bia2.t[:, 0:PT_, :], tot2.t[:, PT_:PT_ + 1, :].to_broadcast([128, PT_, 8]), cal2.t[:, 0:PT_, :], ALU.subtract),
                         reads=[tot2.k, cal2.k], writes=[bia2.k])
                      op("dve", lambda: V.tensor_tensor(bia2.t[0:TS, PT_, :], tot2.t[0:TS, PT_, :], cal2.t[0:TS, PT_, :], ALU.subtract),
                         reads=[tot2.k, cal2.k], writes=[bia2.k])
                      P.mark("bias_done")
                      for grp in range(2):
                          ck = (c_sbk, c_fxk)[grp]; cv = (c_sbv, c_fxv)[grp]
                          kTn = (kTn_s, kTn_f)[grp]
                          for p_ in range(4):
                              cs = slice(p_ * 128, (p_ + 1) * 128)
                              for a_ in range(0, PT_, 8):
                                  a1_ = min(PT_, a_ + 8)
                                  dma("sp", kst.t[:, a_:a1_, :], ck[si][a_ * 128:a1_ * 128, cs].rearrange("(a s) c -> s a c", s=128), kst.k,
                                      writes=[kst.k] if a_ == 0 else [])
                              for a_ in range(0, PT_, 8):
                                  a1_ = min(PT_, a_ + 8)
                                  dma("pool", vp.t[:, a_:a1_, :], cv[si][a_ * 128:a1_ * 128, cs].rearrange("(a s) c -> s a c", s=128), vp.k,
                                      writes=[vp.k] if a_ == 0 else [])
                              dma("sp", vp.t[0:TS, PT_, :], vnew.t[q0:q0 + TS, grp * 512 + p_ * 128:grp * 512 + (p_ + 1) * 128], vp.k,
                                  reads=[vnew.k], writes=[vp.k])
                              a = 0
                              while a < PT_:
                                  pb = nxt(psT, "T")
                                  na = min(4, PT_ - a)
                                  for i in range(na):
                                      op("pe", lambda pb=pb, i=i, a=a: PE.transpose(pb.t[:, i * 128:(i + 1) * 128], kst.t[:, a + i, :], idf.t[:]),
                                         reads=[kst.k, idf.k], writes=[pb.k])
                                  evac(kTp.t[:, a * 128:(a + na) * 128], pb.t[:, 0:na * 128], reads=[pb.k], writes=[kTp.k])
                                  a += na
                              op("dve", lambda p_=p_, kTn=kTn: V.tensor_copy(kTp.t[:, PAST:PAST + TS], kTn.t[:, p_, q0:q0 + TS]), reads=[kTn.k], writes=[kTp.k])
                              if grp == 1:
                                  op("dve", lambda: V.tensor_copy(vpa.t[:, :, :, 0:64], vp.t[:].rearrange("p a (h d) -> p a h d", d=64)),
                                     reads=[vp.k], writes=[vpa.k])
                              P.mark("kv_loaded")
                              for hh in range(2):
                                  h = p_ * 2 + hh
                                  b0 = 64 * hh
                                  P.mark("head")
                                  if grp == 0:
                                      run_lanes([attend_sb(sbsets[0], TS, KL, qTs.t[b0:b0 + 64, p_, q0:q0 + TS], qTs.k, kTp.t[b0:b0 + 64, 0:KL], kTp.k,
                                                lambda a, ks, hh=hh: vp.t[0:ks, a, hh * 64:(hh + 1) * 64], vp.k, Ot.t[0:TS, h * 64:(h + 1) * 64])])
                                  else:
                                      run_lanes([attend_fx(fxset, TS, KL, qTf.t[b0:b0 + 64, p_, q0:q0 + TS], qTf.k, kTp.t[b0:b0 + 64, 0:KL], kTp.k,
                                                lambda a, ks, hh=hh: vpa.t[0:ks, a, hh, :], vpa.k,
                                                lambda a, ks, h=h: bia2.t[0:ks, a, h:h + 1], bia2.k, Ot.t[0:TS, 512 + h * 64:512 + (h + 1) * 64])])
                      dma("sp", O_s[row0 + q0:row0 + q0 + TS, :], Ot.t[0:TS, :], Ot.k, reads=[Ot.k], writes=[t_Os])
                  if bg[0] is not None:
                      for _ in bg[0]:
                          pass
                      bg[0] = None
          P.barrier()

    if NP and "A" in stages:
        stageA(True)
    if NS and "a" in stages:
        stageA(False)

    with ExitStack() as es:
      if "M" in stages:
            psM = [P.ps(es, "psM%d" % i, [128, 512], F32) for i in range(4)]
            psO = [P.ps(es, "psO%d" % i, [128, 512], F32) for i in range(2)]
            psB = [P.ps(es, "psB%d" % i, [128, 1024], BF16) for i in range(2)]
            rr = {"M": 0, "O": 0, "B": 0}

            def nxt(lst, key):
                b = lst[rr[key] % len(lst)]
                rr[key] += 1
                return b

            wo = P.sb(es, "wo", [128, 8, D], BF16, dma=True); wq = P.sb(es, "wq", [128, 8, D], BF16, dma=True)
            wmo = P.sb(es, "wmo", [128, 8, D], BF16, dma=True)
            wk = P.sb(es, "wk", [128, 8, D], BF16, dma=True); wv = P.sb(es, "wv", [128, 8, D], BF16, dma=True)
            gnc = P.sb(es, "gnc", [128, 8], F32)
            load_w_bf16(wo, w_out, D); load_w_bf16(wq, w_mq, D); load_w_bf16(wmo, w_mo, D)
            if NP:
                load_w_bf16(wk, w_mk, D); load_w_bf16(wv, w_mv, D)
            gnr = P.sb(es, "gnr", [8, 128], F32, dma=True)
            dma("sp", gnr.t[:], w_gn.rearrange("(k p) o -> k (p o)", p=128), gnr.k, writes=[gnr.k])
            op("pe", lambda: PE.transpose(psM[0].t[:, 0:8], gnr.t[:], idf.t[0:8, 0:8]), reads=[gnr.k, idf.k], writes=[psM[0].k])
            op("dve", lambda: V.tensor_copy(gnc.t[:], psM[0].t[:, 0:8]), reads=[psM[0].k], writes=[gnc.k])
            for k in range(8):
                op("dve", lambda k=k: V.tensor_scalar(wo.t[:, k, :], wo.t[:, k, :], gnc.t[:, k:k + 1], None, ALU.mult), reads=[wo.k, gnc.k], writes=[wo.k])
            lnp = {}
            for nm, src in (("g1", ln1_g), ("b1", ln1_b), ("g2", ln2_g), ("b2", ln2_b)):
                lnp[nm] = P.sb(es, "ln" + nm, [128, D], F32, dma=True)
                dma("sp", lnp[nm].t[:], src.partition_broadcast(128), lnp[nm].k, writes=[lnp[nm].k])
            memt = P.sb(es, "memt", [128, 2, D], F32, dma=True)
            memv = P.sb(es, "memv", [128, 2, D], F32, dma=True)
            memT = P.sb(es, "memT", [128, 8, NMEM], BF16)
            mkT = P.sb(es, "mkT", [128, 8, NMEM], BF16)
            mva = P.sb(es, "mva", [128, 2, 4, 257], BF16)
            op("pool", lambda: G.memset(mva.t[:, :, :, 256:257], 1.0), writes=[mva.k])
            NLANE = 2
            LB = []
            for L_ in range(NLANE):
                b_ = dict(
                    Oi=P.sb(es, "Oi", [128, D], F32, dma=True), xi=P.sb(es, "xi", [128, D], F32, dma=True),
                    sq2=P.sb(es, "sq2", [128, 512], F32), ssq=P.sb(es, "ssq", [128, 2], F32),
                    On=P.sb(es, "On", [128, D], BF16), OnT=P.sb(es, "OnT", [128, 8, 128], BF16),
                    r1=P.sb(es, "r1", [128, D], F32), x1=P.sb(es, "x1", [128, D], F32),
                    x1b=P.sb(es, "x1b", [128, D], BF16), x1T=P.sb(es, "x1T", [128, 8, 128], BF16),
                    qmT=P.sb(es, "qmT", [128, 8, 128], BF16),
                    PTm=[P.sb(es, "PTm", [128, 128], BF16) for _ in range(2)],
                    Om=P.sb(es, "Om", [128, D], BF16), OmT=P.sb(es, "OmT", [128, 8, 128], BF16),
                    rden=P.sb(es, "rdenm", [128, 1], F32),
                    x2=P.sb(es, "x2", [128, D], F32, dma=True),
                    st=P.sb(es, "lst", [128, 2, 6], F32), mv=P.sb(es, "lmv", [128, 2], F32), rs=P.sb(es, "lrs", [128, 1], F32),
                    pm=[psM[2 * L_], psM[2 * L_ + 1]], po=psO[L_], pb=psB[L_], pmi=[0])
                LB.append(b_)
            mscale = 256 ** -0.5

            def transpose_bf(src, dst):
                pb = nxt(psB, "B")
                for k in range(8):
                    op("pe", lambda pb=pb, k=k: PE.transpose(pb.t[:, k * 128:(k + 1) * 128], src.t[:, k * 128:(k + 1) * 128], idb.t[:]),
                       reads=[src.k, idb.k], writes=[pb.k])
                evac(dst.t[:], pb.t[:].rearrange("p (k e) -> p k e", e=128), reads=[pb.k], writes=[dst.k])

            def prep_mem(si_prompt, si_sample):
                if si_prompt is not None:
                    sq = si_prompt
                    dma("sp", memt.t[:], mem_p[sq].rearrange("(a m) d -> m a d", m=128), memt.k, writes=[memt.k])
                    for a in range(2):
                        for hf in range(2):
                            pb = nxt(psM, "M")
                            for i in range(4):
                                k = hf * 4 + i
                                op("pe", lambda pb=pb, i=i, k=k, a=a: PE.transpose(pb.t[:, i * 128:(i + 1) * 128], memt.t[:, a, k * 128:(k + 1) * 128], idf.t[:]),
                                   reads=[memt.k, idf.k], writes=[pb.k])
                            evac(memT.t[:, hf * 4:(hf + 1) * 4, a * 128:(a + 1) * 128], pb.t[:].rearrange("p (k e) -> p k e", e=128), reads=[pb.k], writes=[memT.k])
                    for (wt_, dstb, outd) in ((wk, memt, o_mk_p), (wv, memv, o_mv_p)):
                        for a in range(2):
                            for c in range(2):
                                pb = nxt(psM, "M")
                                for k in range(8):
                                    op("pe", lambda pb=pb, k=k, a=a, c=c, wt_=wt_: PE.matmul(pb.t[:], memT.t[:, k, a * 128:(a + 1) * 128], wt_.t[:, k, c * 512:(c + 1) * 512],
                                                                                         start=(k == 0), stop=(k == 7)),
                                       reads=[memT.k, wt_.k], writes=[pb.k])
                                evac(dstb.t[:, a, c * 512:(c + 1) * 512], pb.t[:], reads=[pb.k], writes=[dstb.k])
                        dma("sp", outd[sq * NMEM:(sq + 1) * NMEM, :].rearrange("(a m) d -> m a d", m=128), dstb.t[:], t_out, reads=[dstb.k])
                else:
                    dma("sp", memt.t[:], c_mk[si_sample].rearrange("(a m) d -> m a d", m=128), memt.k, writes=[memt.k])
                    dma("sp", memv.t[:], c_mv[si_sample].rearrange("(a m) d -> m a d", m=128), memv.k, writes=[memv.k])
                for a in range(2):
                    for hf in range(2):
                        pb = nxt(psM, "M")
                        for i in range(4):
                            k = hf * 4 + i
                            op("pe", lambda pb=pb, i=i, k=k, a=a: PE.transpose(pb.t[:, i * 128:(i + 1) * 128], memt.t[:, a, k * 128:(k + 1) * 128], idf.t[:]),
                               reads=[memt.k, idf.k], writes=[pb.k])
                        evac(mkT.t[:, hf * 4:(hf + 1) * 4, a * 128:(a + 1) * 128], pb.t[:].rearrange("p (k e) -> p k e", e=128), reads=[pb.k], writes=[mkT.k])
                op("dve", lambda: V.tensor_copy(mva.t[:, :, :, 0:256], memv.t[:].rearrange("p a (h d) -> p a h d", d=256)), reads=[memv.k], writes=[mva.k])

            def ln_gen(B_, r, g_b, b_b, outb):
                st, mv, rs = B_["st"], B_["mv"], B_["rs"]
                for c in range(2):
                    op("dve", lambda c=c: V.bn_stats(st.t[:, c, :], r.t[:, c * 512:(c + 1) * 512]), reads=[r.k], writes=[st.k])
                yield
                op("dve", lambda: V.bn_aggr(mv.t[:], st.t[:]), reads=[st.k], writes=[mv.k])
                op("dve", lambda: V.tensor_scalar(rs.t[:], mv.t[:, 1:2], LN_EPS, None, ALU.add), reads=[mv.k], writes=[rs.k])
                yield
                op("act", lambda: A.sqrt(rs.t[:], rs.t[:]), reads=[rs.k], writes=[rs.k])
                yield
                op("dve", lambda: V.reciprocal(rs.t[:], rs.t[:]), reads=[rs.k], writes=[rs.k])
                yield
                op("dve", lambda: V.tensor_scalar(r.t[:], r.t[:], mv.t[:, 0:1], rs.t[:], ALU.subtract, ALU.mult),
                   reads=[r.k, mv.k, rs.k], writes=[r.k])
                yield
                op("dve", lambda: V.tensor_mul(r.t[:], r.t[:], g_b.t[:]), reads=[r.k, g_b.k], writes=[r.k])
                yield
                op("dve", lambda: V.tensor_add(outb.t[:], r.t[:], b_b.t[:]), reads=[r.k, b_b.k], writes=[outb.k])
                yield

            def pmn(B_):
                b = B_["pm"][B_["pmi"][0] % 2]
                B_["pmi"][0] += 1
                return b

            def tr_gen(B_, src, dst):
                pb = B_["pb"]
                for k in range(8):
                    op("pe", lambda k=k: PE.transpose(pb.t[:, k * 128:(k + 1) * 128], src.t[:, k * 128:(k + 1) * 128], idb.t[:]),
                       reads=[src.k, idb.k], writes=[pb.k])
                yield
                op("act", lambda: A.copy(dst.t[:], pb.t[:].rearrange("p (k e) -> p k e", e=128)), reads=[pb.k], writes=[dst.k])
                yield

            def front_gen(B_, row0, x_rows):
                Oi, xi, sq2, ssq, On, OnT, r1, x1, x1b, x1T, qmT = (B_[k] for k in ("Oi", "xi", "sq2", "ssq", "On", "OnT", "r1", "x1", "x1b", "x1T", "qmT"))
                dma("sp", Oi.t[:], O_s[row0:row0 + 128, :], Oi.k, reads=[t_Os], writes=[Oi.k])
                dma("sp", xi.t[:], x_rows, xi.k, writes=[xi.k])
                yield
                for g_ in range(2):
                    op("act", lambda g_=g_: A.activation(sq2.t[:], Oi.t[:, g_ * 512:(g_ + 1) * 512], AF.Square, accum_out=ssq.t[:, g_:g_ + 1]),
                       reads=[Oi.k], writes=[sq2.k, ssq.k])
                yield
                op("dve", lambda: V.tensor_scalar(ssq.t[:], ssq.t[:], 1.0 / 512, GN_EPS, ALU.mult, ALU.add), reads=[ssq.k], writes=[ssq.k])
                yield
                op("act", lambda: A.sqrt(ssq.t[:], ssq.t[:]), reads=[ssq.k], writes=[ssq.k])
                yield
                op("dve", lambda: V.reciprocal(ssq.t[:], ssq.t[:]), reads=[ssq.k], writes=[ssq.k])
                yield
                for g_ in range(2):
                    op("dve", lambda g_=g_: V.tensor_scalar(On.t[:, g_ * 512:(g_ + 1) * 512], Oi.t[:, g_ * 512:(g_ + 1) * 512], ssq.t[:, g_:g_ + 1], None, ALU.mult),
                       reads=[Oi.k, ssq.k], writes=[On.k])
                yield
                yield from tr_gen(B_, On, OnT)
                for c in range(2):
                    pb = pmn(B_)
                    for k in range(8):
                        op("pe", lambda pb=pb, k=k, c=c: PE.matmul(pb.t[:], OnT.t[:, k, :], wo.t[:, k, c * 512:(c + 1) * 512], start=(k == 0), stop=(k == 7)),
                           reads=[OnT.k, wo.k], writes=[pb.k])
                    yield
                    op("dve", lambda pb=pb, c=c: V.scalar_tensor_tensor(r1.t[:, c * 512:(c + 1) * 512], xi.t[:, c * 512:(c + 1) * 512], ALPHA, pb.t[:], ALU.mult, ALU.add),
                       reads=[xi.k, pb.k], writes=[r1.k])
                yield
                yield from ln_gen(B_, r1, lnp["g1"], lnp["b1"], x1)
                op("act", lambda: A.copy(x1b.t[:], x1.t[:]), reads=[x1.k], writes=[x1b.k])
                yield
                yield from tr_gen(B_, x1b, x1T)
                for hf in range(2):
                    pb = pmn(B_)
                    for i in range(4):
                        c = hf * 4 + i
                        for k in range(8):
                            op("pe", lambda pb=pb, i=i, c=c, k=k: PE.matmul(pb.t[:, i * 128:(i + 1) * 128], wq.t[:, k, c * 128:(c + 1) * 128], x1T.t[:, k, :],
                                                                           start=(k == 0), stop=(k == 7)),
                               reads=[wq.k, x1T.k], writes=[pb.k])
                    yield
                    op("act", lambda pb=pb, hf=hf: A.copy(qmT.t[:, hf * 4:(hf + 1) * 4, :], pb.t[:].rearrange("p (k e) -> p k e", e=128)), reads=[pb.k], writes=[qmT.k])
                yield

            def attn_gen(B_, q0, nq):
                qmT, PTm, Om, rden, po = B_["qmT"], B_["PTm"], B_["Om"], B_["rden"], B_["po"]
                for h in range(4):
                    for a in range(2):
                        pb = pmn(B_); pt = PTm[a]
                        for cc in range(2):
                            op("pe", lambda pb=pb, a=a, cc=cc, h=h: PE.matmul(pb.t[:, 0:nq], mkT.t[:, 2 * h + cc, a * 128:(a + 1) * 128], qmT.t[:, 2 * h + cc, q0:q0 + nq],
                                                                            start=(cc == 0), stop=(cc == 1)),
                               reads=[mkT.k, qmT.k], writes=[pb.k])
                        yield
                        op("act", lambda pb=pb, pt=pt: A.activation(pt.t[:, 0:nq], pb.t[:, 0:nq], AF.Exp, scale=mscale), reads=[pb.k], writes=[pt.k])
                        yield
                        op("pe", lambda pt=pt, a=a, h=h: PE.matmul(po.t[q0:q0 + nq, 0:257], pt.t[:, 0:nq], mva.t[:, a, h, :], start=(a == 0), stop=(a == 1)),
                           reads=[pt.k, mva.k], writes=[po.k])
                    yield
                    op("dve", lambda: V.reciprocal(rden.t[q0:q0 + nq, :], po.t[q0:q0 + nq, 256:257]), reads=[po.k], writes=[rden.k])
                    op("dve", lambda h=h: V.tensor_scalar(Om.t[q0:q0 + nq, h * 256:(h + 1) * 256], po.t[q0:q0 + nq, 0:256], rden.t[q0:q0 + nq, :], None, ALU.mult),
                       reads=[po.k, rden.k], writes=[Om.k])
                    yield

            def back_gen(B_, row0):
                Om, OmT, x1, x2, r2 = B_["Om"], B_["OmT"], B_["x1"], B_["x2"], B_["r1"]
                yield from tr_gen(B_, Om, OmT)
                for c in range(2):
                    pb = pmn(B_)
                    for k in range(8):
                        op("pe", lambda pb=pb, k=k, c=c: PE.matmul(pb.t[:], OmT.t[:, k, :], wmo.t[:, k, c * 512:(c + 1) * 512], start=(k == 0), stop=(k == 7)),
                           reads=[OmT.k, wmo.k], writes=[pb.k])
                    yield
                    op("dve", lambda pb=pb, c=c: V.scalar_tensor_tensor(r2.t[:, c * 512:(c + 1) * 512], x1.t[:, c * 512:(c + 1) * 512], ALPHA, pb.t[:], ALU.mult, ALU.add),
                       reads=[x1.k, pb.k], writes=[r2.k])
                yield
                yield from ln_gen(B_, r2, lnp["g2"], lnp["b2"], x2)
                dma("pool", x2_s[row0:row0 + 128, :], x2.t[:], x2.k, reads=[x2.k], writes=[t_x2s])
                yield

            def tile_gen(B_, row0, x_rows):
                yield from front_gen(B_, row0, x_rows)
                yield from attn_gen(B_, 0, 128)
                yield from back_gen(B_, row0)

            def chain(gens):
                for g_ in gens:
                    yield from g_

            def run_lanes(lanes):
                lanes = list(lanes)
                while lanes:
                    for l in list(lanes):
                        try:
                            next(l)
                        except StopIteration:
                            lanes.remove(l)

            for sq in range(NP):
                prep_mem(sq, None)
                run_lanes([chain([tile_gen(LB[L_], sq * T + j * 128, x_p[sq, j * 128:(j + 1) * 128, :]) for j in range(L_, KT, NLANE)]) for L_ in range(NLANE)])
            if NS:
                row0 = NP * T
                for _ in front_gen(LB[0], row0, x_s[:, :]):
                    pass
                for si in range(NS):
                    prep_mem(None, si)
                    for _ in attn_gen(LB[0], si * TS, TS):
                        pass
                for _ in back_gen(LB[0], row0):
                    pass
            P.barrier()

    with ExitStack() as es:
      if "B" in stages:
            NTB = 2
            psA = [P.ps(es, "psA%d" % i, [128, 512], F32) for i in range(4)]
            psH = [P.ps(es, "psH%d" % i, [128, 512], F32) for i in range(2)]
            psG = [P.ps(es, "psG%d" % i, [128, 512], F32) for i in range(2)]
            rr = {"H": 0, "G": 0}

            def nxt(lst, key):
                b = lst[rr[key] % len(lst)]
                rr[key] += 1
                return b

            wpq = P.sb(es, "wpq", [128, 8, 2048], BF16, dma=True)
            load_w_bf16(wpq, w_pq, 2048)
            kT2 = P.sb(es, "kT2", [128, 16, 128], BF16)
            with ExitStack() as est:
                kst2 = P.sb(est, "kst2", [128, 16, 128], F32, dma=True)
                for hh_ in range(0, 8, 4):
                    dma("sp", kst2.t[:, hh_:hh_ + 4, :], keys_a[hh_:hh_ + 4].rearrange("h k d -> k h d"), kst2.k, writes=[kst2.k] if hh_ == 0 else [])
                for hh_ in range(0, 8, 4):
                    dma("sp", kst2.t[:, 8 + hh_:8 + hh_ + 4, :], keys_b[hh_:hh_ + 4].rearrange("h k d -> k h d"), kst2.k, writes=[])
                for q4 in range(4):
                    pb = nxt(psG, "G")
                    for i in range(4):
                        op("pe", lambda pb=pb, i=i, q4=q4: PE.transpose(pb.t[:, i * 128:(i + 1) * 128], kst2.t[:, q4 * 4 + i, :], idf.t[:]),
                           reads=[kst2.k, idf.k], writes=[pb.k])
                    evac(kT2.t[:, q4 * 4:(q4 + 1) * 4, :], pb.t[:].rearrange("p (k e) -> p k e", e=128), reads=[pb.k], writes=[kT2.k])
                P.barrier()
            g3 = P.sb(es, "g3", [128, D], F32, dma=True); b3 = P.sb(es, "b3", [128, D], F32, dma=True)
            dma("sp", g3.t[:], ln3_g.partition_broadcast(128), g3.k, writes=[g3.k])
            dma("sp", b3.t[:], ln3_b.partition_broadcast(128), b3.k, writes=[b3.k])
            NTMAX = NTB * 128
            x2tq = [[P.sb(es, "x2t", [128, D], F32, dma=True) for _ in range(NTB)] for _q in range(2)]
            x2Tq = [P.sb(es, "x2T", [128, 8, NTMAX], BF16) for _q in range(2)]
            qpT = P.sb(es, "qpT", [128, 16, 128], BF16)
            SC = P.sb(es, "SC", [128, 16, 128], F32)
            TMP = P.sb(es, "TMP", [128, 16, 128], F32)
            VAB = P.sb(es, "VAB", [128, 16, 16], F32)
            IAB = P.sb(es, "IAB", [128, 16, 16], U32)
            IABf = P.sb(es, "IABf", [128, 16, 16], F32)
            TOP = P.sb(es, "TOP", [128, 8, 16], F32)
            POS = P.sb(es, "POS", [128, 8, 16], U32)
            POSa = P.sb(es, "POSa", [128, 8, 16], U32); POSb = P.sb(es, "POSb", [128, 8, 16], U32)
            POSaf = P.sb(es, "POSaf", [128, 8, 16], F32); POSbf = P.sb(es, "POSbf", [128, 8, 16], F32)
            KAf = P.sb(es, "KAf", [128, 8, 16], F32); KBf = P.sb(es, "KBf", [128, 8, 16], F32)
            Gt = P.sb(es, "Gt", [128, 8, 16], F32)
            zsum = P.sb(es, "zsum", [128, 8], F32)
            KATq = [[P.sb(es, "KAT", [128, 128], BF16) for _ in range(NTB)] for _q in range(2)]
            KBTq = [[P.sb(es, "KBT", [128, 128], BF16) for _ in range(NTB)] for _q in range(2)]
            GTTq = [[P.sb(es, "GTT", [128, 128], BF16) for _ in range(NTB)] for _q in range(2)]
            TB = 8
            PA = [P.sb(es, "PA", [128, TB, 128], BF16) for _ in range(2)]
            PB = [P.sb(es, "PB", [128, TB, 128], BF16) for _ in range(2)]
            GTall = P.sb(es, "GTall", [128, NTMAX, 128], BF16)
            iota_b = P.sb(es, "iota_b", [128, 128], BF16)
            op("dve", lambda: V.tensor_copy(iota_b.t[:], iota_f.t[:]), reads=[iota_f.k], writes=[iota_b.k])
            NBK = 2
            NUB = 3
            ubuf = [P.sb(es, "ubuf", [128, NBK, 1024], BF16, dma=True) for _ in range(NUB)]
            vbuf = [P.sb(es, "vbuf", [128, NBK, 1024], BF16, dma=True) for _ in range(NUB)]
            NHB = 3
            hg = [P.sb(es, "hg", [128, NTMAX], BF16) for _ in range(NHB)]
            hgg = [P.sb(es, "hgg", [128, NTMAX], BF16) for _ in range(NHB)]
            r3 = Buf(SC.t[:].rearrange("p a k -> p (a k)")[:, 0:D], SC.k)
            candv = SC.t[:].rearrange("p (h a) k -> p h (a k)", a=2)
            tmpCv = TMP.t[:].rearrange("p (h a) k -> p h (a k)", a=2)
            eqwv = TMP.t[:].rearrange("p (h a) (r i) -> p h (a r) i", a=2, i=16)

            rbank = [psG[1]]

            def route_gen(q, ti, r0):
                x2tile = x2tq[q][ti]; x2T = x2Tq[q]
                KAT, KBT, GTT = KATq[q][ti], KBTq[q][ti], GTTq[q][ti]
                tsl = slice(ti * 128, (ti + 1) * 128)
                dma("sp", x2tile.t[:], x2_s[r0:r0 + 128, :], x2tile.k, reads=[t_x2s], writes=[x2tile.k])
                for hf in range(2):
                    pb = rbank[0]
                    for i in range(4):
                        k = hf * 4 + i
                        op("pe", lambda pb=pb, i=i, k=k: PE.transpose(pb.t[:, i * 128:(i + 1) * 128], x2tile.t[:, k * 128:(k + 1) * 128], idf.t[:]),
                           reads=[x2tile.k, idf.k], writes=[pb.k])
                    op("act", lambda pb=pb: A.copy(x2T.t[:, hf * 4:(hf + 1) * 4, tsl], pb.t[:].rearrange("p (k e) -> p k e", e=128)), reads=[pb.k], writes=[x2T.k])
                    yield
                for q4 in range(4):
                    pb = rbank[0]
                    for i in range(4):
                        c = q4 * 4 + i
                        for k in range(8):
                            op("pe", lambda pb=pb, i=i, c=c, k=k: PE.matmul(pb.t[:, i * 128:(i + 1) * 128], wpq.t[:, k, c * 128:(c + 1) * 128], x2T.t[:, k, tsl],
                                                                           start=(k == 0), stop=(k == 7)),
                               reads=[wpq.k, x2T.k], writes=[pb.k])
                    op("act", lambda pb=pb: A.copy(qpT.t[:, q4 * 4:(q4 + 1) * 4, :], pb.t[:].rearrange("p (k e) -> p k e", e=128)), reads=[pb.k], writes=[qpT.k])
                    yield
                for q4 in range(4):
                    pb = rbank[0]
                    for i in range(4):
                        s_ = q4 * 4 + i
                        half, h = s_ // 8, s_ % 8
                        op("pe", lambda pb=pb, i=i, s_=s_, half=half, h=h: PE.matmul(pb.t[:, i * 128:(i + 1) * 128], qpT.t[:, h * 2 + half, :], kT2.t[:, s_, :],
                                                                                     start=True, stop=True),
                           reads=[qpT.k, kT2.k], writes=[pb.k])
                    op("act", lambda pb=pb: A.copy(SC.t[:, q4 * 4:(q4 + 1) * 4, :], pb.t[:].rearrange("p (k e) -> p k e", e=128)), reads=[pb.k], writes=[SC.k])
                    yield
                for s_ in range(16):
                    op("dve", lambda s_=s_: V.max(VAB.t[:, s_, 0:8], SC.t[:, s_, :]), reads=[SC.k], writes=[VAB.k])
                    if s_ % 2 == 1:
                        yield
                for s_ in range(16):
                    op("dve", lambda s_=s_: V.match_replace(TMP.t[:, s_, :], VAB.t[:, s_, 0:8], SC.t[:, s_, :], -1e30), reads=[SC.k, VAB.k], writes=[TMP.k])
                    if s_ % 2 == 1:
                        yield
                for s_ in range(16):
                    op("dve", lambda s_=s_: V.max(VAB.t[:, s_, 8:16], TMP.t[:, s_, :]), reads=[TMP.k], writes=[VAB.k])
                    if s_ % 2 == 1:
                        yield
                for s_ in range(16):
                    op("dve", lambda s_=s_: V.max_index(IAB.t[:, s_, 0:8], VAB.t[:, s_, 0:8], SC.t[:, s_, :]), reads=[SC.k, VAB.k], writes=[IAB.k])
                    if s_ % 2 == 1:
                        yield
                for s_ in range(16):
                    op("dve", lambda s_=s_: V.max_index(IAB.t[:, s_, 8:16], VAB.t[:, s_, 8:16], TMP.t[:, s_, :]), reads=[TMP.k, VAB.k], writes=[IAB.k])
                    if s_ % 2 == 1:
                        yield
                op("dve", lambda: V.tensor_copy(IABf.t[:], IAB.t[:]), reads=[IAB.k], writes=[IABf.k])
                op("dve", lambda: V.tensor_tensor(candv.rearrange("p h (i j) -> p h i j", j=16),
                                                  VAB.t[:, 0:8, :].unsqueeze(3).to_broadcast([128, 8, 16, 16]),
                                                  VAB.t[:, 8:16, :].unsqueeze(2).to_broadcast([128, 8, 16, 16]), ALU.add),
                   reads=[VAB.k], writes=[SC.k])
                for h in range(8):
                    op("dve", lambda h=h: V.max(TOP.t[:, h, 0:8], candv[:, h, :]), reads=[SC.k], writes=[TOP.k])
                    if h % 2 == 1:
                        yield
                for h in range(8):
                    op("dve", lambda h=h: V.match_replace(tmpCv[:, h, :], TOP.t[:, h, 0:8], candv[:, h, :], -1e30), reads=[SC.k, TOP.k], writes=[TMP.k])
                    if h % 2 == 1:
                        yield
                for h in range(8):
                    op("dve", lambda h=h: V.max(TOP.t[:, h, 8:16], tmpCv[:, h, :]), reads=[TMP.k], writes=[TOP.k])
                    if h % 2 == 1:
                        yield
                for h in range(8):
                    op("dve", lambda h=h: V.max_index(POS.t[:, h, 0:8], TOP.t[:, h, 0:8], candv[:, h, :]), reads=[SC.k, TOP.k], writes=[POS.k])
                    if h % 2 == 1:
                        yield
                for h in range(8):
                    op("dve", lambda h=h: V.max_index(POS.t[:, h, 8:16], TOP.t[:, h, 8:16], tmpCv[:, h, :]), reads=[TMP.k, TOP.k], writes=[POS.k])
                    if h % 2 == 1:
                        yield
                op("dve", lambda: V.tensor_single_scalar(POSa.t[:], POS.t[:], 4, ALU.logical_shift_right), reads=[POS.k], writes=[POSa.k])
                op("dve", lambda: V.tensor_single_scalar(POSb.t[:], POS.t[:], 15, ALU.bitwise_and), reads=[POS.k], writes=[POSb.k])
                op("dve", lambda: V.tensor_copy(POSaf.t[:], POSa.t[:]), reads=[POSa.k], writes=[POSaf.k])
                op("dve", lambda: V.tensor_copy(POSbf.t[:], POSb.t[:]), reads=[POSb.k], writes=[POSbf.k])
                op("dve", lambda: V.tensor_tensor(Gt.t[:], TOP.t[:], TOP.t[:, :, 0:1].to_broadcast([128, 8, 16]), ALU.subtract), reads=[TOP.k], writes=[Gt.k])
                for (posf, half, dst) in ((POSaf, 0, KAf), (POSbf, 1, KBf)):
                    op("dve", lambda posf=posf: V.tensor_tensor(eqwv, iota_f.t[:, 0:16].unsqueeze(1).unsqueeze(1).to_broadcast([128, 8, 16, 16]),
                                                                posf.t[:].unsqueeze(3).to_broadcast([128, 8, 16, 16]), ALU.is_equal),
                       reads=[iota_f.k, posf.k], writes=[TMP.k])
                    op("dve", lambda half=half: V.tensor_tensor(eqwv, eqwv, IABf.t[:, half * 8:(half + 1) * 8, :].unsqueeze(2).to_broadcast([128, 8, 16, 16]), ALU.mult),
                       reads=[TMP.k, IABf.k], writes=[TMP.k])
                    op("dve", lambda dst=dst: V.tensor_reduce(dst.t[:], eqwv, AX.X, ALU.add), reads=[TMP.k], writes=[dst.k])
                for _ in range(6):
                    yield
                op("act", lambda: A.activation(Gt.t[:], Gt.t[:], AF.Exp), reads=[Gt.k], writes=[Gt.k])
                for _ in range(4):
                    yield
                op("dve", lambda: V.tensor_reduce(zsum.t[:], Gt.t[:], AX.X, ALU.add), reads=[Gt.k], writes=[zsum.k])
                op("dve", lambda: V.reciprocal(zsum.t[:], zsum.t[:]), reads=[zsum.k], writes=[zsum.k])
                op("dve", lambda: V.tensor_tensor(Gt.t[:], Gt.t[:], zsum.t[:].unsqueeze(2).to_broadcast([128, 8, 16]), ALU.mult), reads=[Gt.k, zsum.k], writes=[Gt.k])
                for _ in range(8):
                    yield
                pb = rbank[0]
                for i, src in enumerate((KAf, KBf, Gt)):
                    op("pe", lambda pb=pb, i=i, src=src: PE.transpose(pb.t[:, i * 128:(i + 1) * 128], src.t[:].rearrange("p h r -> p (h r)"), idf.t[:]),
                       reads=[src.k, idf.k], writes=[pb.k])
                op("act", lambda pb=pb: A.copy(KAT.t[:], pb.t[:, 0:128]), reads=[pb.k], writes=[KAT.k])
                op("act", lambda pb=pb: A.copy(KBT.t[:], pb.t[:, 128:256]), reads=[pb.k], writes=[KBT.k])
                op("act", lambda pb=pb: A.copy(GTT.t[:], pb.t[:, 256:384]), reads=[pb.k], writes=[GTT.k])
            gbanks = [psH[0], psH[1], psG[0], psG[1]]
            gbi = [0]

            def gbuild(q, ti):
                KAT, KBT, GTT = KATq[q][ti], KBTq[q][ti], GTTq[q][ti]
                iob = iota_b.t[:].unsqueeze(1).to_broadcast([128, TB, 128])

                def stage1(nb):
                    pa = PA[nb % 2]; pbt = PB[nb % 2]
                    n0 = nb * TB
                    op("dve", lambda: V.tensor_tensor(pa.t[:], iob, KAT.t[:, n0:n0 + TB].unsqueeze(2).to_broadcast([128, TB, 128]), ALU.is_equal),
                       reads=[iota_b.k, KAT.k], writes=[pa.k])
                    if nb % 3 == 2:
                        op("dve", lambda: V.tensor_tensor(pa.t[:], pa.t[:], GTT.t[:, n0:n0 + TB].unsqueeze(2).to_broadcast([128, TB, 128]), ALU.mult),
                           reads=[pa.k, GTT.k], writes=[pa.k])
                    else:
                        op("pool", lambda: G.tensor_tensor(pa.t[:], pa.t[:], GTT.t[:, n0:n0 + TB].unsqueeze(2).to_broadcast([128, TB, 128]), ALU.mult),
                           reads=[pa.k, GTT.k], writes=[pa.k])
                    op("dve", lambda: V.tensor_tensor(pbt.t[:], iob, KBT.t[:, n0:n0 + TB].unsqueeze(2).to_broadcast([128, TB, 128]), ALU.is_equal),
                       reads=[iota_b.k, KBT.k], writes=[pbt.k])

                def stage2(nb):
                    pa = PA[nb % 2]; pbt = PB[nb % 2]
                    n0 = nb * TB
                    for q_ in range(TB // 4):
                        pg = gbanks[gbi[0] % 4]; gbi[0] += 1
                        for i in range(4):
                            op("pe", lambda pg=pg, i=i, q_=q_: PE.matmul(pg.t[:, i * 128:(i + 1) * 128], pbt.t[:, q_ * 4 + i, :], pa.t[:, q_ * 4 + i, :], start=True, stop=True),
                               reads=[pa.k, pbt.k], writes=[pg.k])
                        c0 = ti * 128 + n0 + q_ * 4
                        op("act", lambda pg=pg, c0=c0: A.copy(GTall.t[:, c0:c0 + 4, :], pg.t[:].rearrange("p (n a) -> p n a", a=128)), reads=[pg.k], writes=[GTall.k])

                ngr = 128 // TB
                stage1(0)
                for nb in range(ngr):
                    if nb + 1 < ngr:
                        stage1(nb + 1)
                    stage2(nb)

            tile_rows = [(t * 128) for t in range(NTILE)]
            hbanks = [psH[0], psH[1], psG[0]]
            LA = 2

            def load_u(g):
                sl = g % NUB
                dma("sp", ubuf[sl].t[:], uT_s[g * NBK:(g + 1) * NBK].rearrange("b p f -> p b f"), ubuf[sl].k, reads=[t_uTs], writes=[ubuf[sl].k])

            def load_v(g):
                sl = g % NUB
                dma("sp", vbuf[sl].t[:], v_s[g * NBK * 128:(g + 1) * NBK * 128, :].rearrange("(b e) d -> e b d", e=128), vbuf[sl].k, reads=[t_vs], writes=[vbuf[sl].k])

            def epilogue_gen(pidx):
                p0, ntp = passes[pidx]
                st = lnw["st"]; mv = lnw["mv"]; rs = lnw["rs"]
                for ti in range(ntp):
                    r = x2tq[pidx % 2][ti]
                    r0 = tile_rows[p0 + ti]
                    for c in range(2):
                        op("dve", lambda c=c, r=r: V.bn_stats(st.t[:, c, :], r.t[:, c * 512:(c + 1) * 512]), reads=[r.k], writes=[st.k])
                    yield
                    op("dve", lambda: V.bn_aggr(mv.t[:], st.t[:]), reads=[st.k], writes=[mv.k])
                    op("dve", lambda: V.tensor_scalar(rs.t[:], mv.t[:, 1:2], LN_EPS, None, ALU.add), reads=[mv.k], writes=[rs.k])
                    yield
                    op("act", lambda: A.sqrt(rs.t[:], rs.t[:]), reads=[rs.k], writes=[rs.k])
                    yield
                    op("dve", lambda: V.reciprocal(rs.t[:], rs.t[:]), reads=[rs.k], writes=[rs.k])
                    yield
                    op("dve", lambda r=r: V.tensor_scalar(r.t[:], r.t[:], mv.t[:, 0:1], rs.t[:], ALU.subtract, ALU.mult),
                       reads=[r.k, mv.k, rs.k], writes=[r.k])
                    yield
                    op("dve", lambda r=r: V.tensor_mul(r.t[:], r.t[:], g3.t[:]), reads=[r.k, g3.k], writes=[r.k])
                    yield
                    op("dve", lambda r=r: V.tensor_add(r.t[:], r.t[:], b3.t[:]), reads=[r.k, b3.k], writes=[r.k])
                    yield
                    if r0 < NP * T:
                        dma("sp", y_p[r0:r0 + 128, :], r.t[:], r.k, reads=[r.k])
                    else:
                        dma("sp", y_s[r0 - NP * T:r0 - NP * T + 128, :], r.t[:], r.k, reads=[r.k])

            passes = []
            pi = 0
            while pi < NTILE:
                nt = min(NTB, NTILE - pi)
                passes.append((pi, nt))
                pi += nt

            def route_pass(pidx):
                p0, ntp = passes[pidx]
                for ti in range(ntp):
                    yield from route_gen(pidx % 2, ti, tile_rows[p0 + ti])

            for _ in route_pass(0):
                pass
            NG = 128 // NBK
            for pidx, (pi, nt) in enumerate(passes):
                q = pidx % 2
                NT = nt * 128
                x2T = x2Tq[q]; x2t = x2tq[q]
                for ti in range(nt):
                    gbuild(q, ti)
                def lane_gen(pidx=pidx):
                    if pidx >= 1:
                        yield from epilogue_gen(pidx - 1)
                    if pidx + 1 < len(passes):
                        yield from route_pass(pidx + 1)
                nxt_route = lane_gen()
                load_u(0); load_u(1); load_v(0)

                def emit_hd(ka):
                    g, b = ka // NBK, ka % NBK
                    sl = g % NUB
                    ph = hbanks[ka % len(hbanks)]; hb = hg[ka % NHB]; hgb = hgg[ka % NHB]
                    for k in range(8):
                        op("pe", lambda ph=ph, k=k, b=b, sl=sl: PE.matmul(ph.t[:, 0:NT], ubuf[sl].t[:, b, k * 128:(k + 1) * 128], x2T.t[:, k, 0:NT], start=(k == 0), stop=(k == 7)),
                           reads=[ubuf[sl].k, x2T.k], writes=[ph.k])
                    op("act", lambda ph=ph, hb=hb: A.activation(hb.t[:, 0:NT], ph.t[:, 0:NT], AF.Gelu), reads=[ph.k], writes=[hb.k])
                    op("pool", lambda hb=hb, hgb=hgb, ka=ka: G.tensor_mul(hgb.t[:, 0:NT], hb.t[:, 0:NT], GTall.t[:, 0:NT, ka]), reads=[hb.k, GTall.k], writes=[hgb.k])

                def emit_v(ka):
                    g, b = ka // NBK, ka % NBK
                    sl = g % NUB
                    hgb = hgg[ka % NHB]
                    for ti in range(nt):
                        for c in range(2):
                            pa_ = psA[ti * 2 + c]
                            op("pe", lambda pa_=pa_, hgb=hgb, ti=ti, c=c, b=b, sl=sl: PE.matmul(pa_.t[:], hgb.t[:, ti * 128:(ti + 1) * 128], vbuf[sl].t[:, b, c * 512:(c + 1) * 512],
                                                                                         start=(ka == 0), stop=(ka == 127)),
                               reads=[hgb.k, vbuf[sl].k], writes=[pa_.k])

                assert NBK == 2
                for step in range(128 + LA):
                    if step % NBK == 0:
                        gq = step // NBK
                        if gq + 2 < NG:
                            load_u(gq + 2)
                        if gq + 1 < NG:
                            load_v(gq + 1)
                    if step < 128:
                        emit_hd(step)
                    kv = step - LA
                    if kv >= 0:
                        emit_v(kv)
                    for _ in range(2 if step % 2 else 1):
                        next(nxt_route, None)
                for _ in nxt_route:
                    pass
                for ti in range(nt):
                    for c in range(2):
                        pa_ = psA[ti * 2 + c]
                        op("dve", lambda pa_=pa_, ti=ti, c=c: V.scalar_tensor_tensor(x2t[ti].t[:, c * 512:(c + 1) * 512], x2t[ti].t[:, c * 512:(c + 1) * 512], ALPHA, pa_.t[:], ALU.mult, ALU.add),
                           reads=[x2t[ti].k, pa_.k], writes=[x2t[ti].k])
            for _ in epilogue_gen(len(passes) - 1):
                pass
            P.barrier()
    return nc, P


_IN_NAMES = None


def shard_inputs(inp, NP, NS, ncores):
    maps = []
    f = np.ascontiguousarray
    for c in range(ncores):
        ps_ = slice(c * NP, (c + 1) * NP)
        ss_ = slice(c * NS, (c + 1) * NS)
        m = {
            "x_p": f(inp["x_prompt"][ps_]),
            "x_s": f(inp["x_sample"][ss_].reshape(-1, D)),
            "mem_p": f(inp["mem_prompt"][ps_]),
            "c_sbk": f(inp["cache_sb_k"][0, ss_].reshape(NS, -1, 512)),
            "c_sbv": f(inp["cache_sb_v"][0, ss_].reshape(NS, -1, 512)),
            "c_fxk": f(inp["cache_fox_k"][0, ss_].reshape(NS, -1, 512)),
            "c_fxv": f(inp["cache_fox_v"][0, ss_].reshape(NS, -1, 512)),
            "c_lf": f(inp["cache_fox_logf"][0, ss_]),
            "c_mk": f(inp["cache_mem_k"][0, ss_].reshape(NS, NMEM, D)),
            "c_mv": f(inp["cache_mem_v"][0, ss_].reshape(NS, NMEM, D)),
            "w_in": f(inp["w_in"][0]), "b_f": f(inp["b_f"][0].reshape(1, 8)), "w_gn": f(inp["w_gn"][0].reshape(D, 1)),
            "w_out": f(inp["w_out"][0]), "ln1_g": f(inp["ln1_g"][0].reshape(1, D)), "ln1_b": f(inp["ln1_b"][0].reshape(1, D)),
            "w_mq": f(inp["w_mq"][0]), "w_mk": f(inp["w_mk"][0]), "w_mv": f(inp["w_mv"][0]), "w_mo": f(inp["w_mo"][0]),
            "ln2_g": f(inp["ln2_g"][0].reshape(1, D)), "ln2_b": f(inp["ln2_b"][0].reshape(1, D)),
            "w_pq": f(inp["w_pq"][0]), "keys_a": f(inp["peer_keys_a"][0]), "keys_b": f(inp["peer_keys_b"][0]),
            "pu": f(inp["peer_u"][0]), "pv": f(inp["peer_v"][0]),
            "ln3_g": f(inp["ln3_g"][0].reshape(1, D)), "ln3_b": f(inp["ln3_b"][0].reshape(1, D)),
        }
        maps.append({k: np.asarray(v, dtype=np.float32) for k, v in m.items()})
    return maps


def gather_outputs(results, NP, T, NS, TS):
    def cat(name, shape_per_core):
        return np.concatenate([np.asarray(r[name]).reshape(shape_per_core) for r in results], axis=0)
    B = NP * len(results)
    Bs = NS * len(results)
    y_p = cat("y_p", (NP, T, D)); y_s = cat("y_s", (NS, TS, D))
    outs = [y_p, y_s]
    for nm in ("o_sbk_p", "o_sbv_p", "o_fxk_p", "o_fxv_p"):
        outs.append(cat(nm, (NP, T, 8, HD))[None])
    outs.append(cat("o_lf_p", (NP, T, 8))[None])
    for nm in ("o_mk_p", "o_mv_p"):
        outs.append(cat(nm, (NP, NMEM, 4, 256))[None])
    for nm in ("o_sbk_s", "o_sbv_s", "o_fxk_s", "o_fxv_s"):
        outs.append(cat(nm, (NS, TS, 8, HD))[None])
    outs.append(cat("o_lf_s", (NS, TS, 8))[None])
    return tuple(np.ascontiguousarray(o, dtype=np.float32) for o in outs)


def kernel(**inputs):
    ncores = 8
    B, T, _ = inputs["x_prompt"].shape
    Bs, TS, _ = inputs["x_sample"].shape
    PAST = inputs["cache_sb_k"].shape[2]
    NP, NS = B // ncores, Bs // ncores
    nc, _ = build(NP, T, NS, TS, PAST)
    maps = shard_inputs(inputs, NP, NS, ncores)
    res = run_bass_kernel_spmd(nc, maps, core_ids=list(range(ncores)))
    return gather_outputs(res.results, NP, T, NS, TS)
```
